# Optimizing a Trainium2 kernel written in Bass

```python
import math, functools
import jax, jax.numpy as jnp
from jax import lax
import numpy as np

D_MODEL = 1024
BATCH = 32
SEQ = 256
DEPTH = 2
DEC_BATCH = 8
DEC_SEQ = 2048
PAST_LEN = 512

GRID_W = 64
HEAD_DIM = 64
N_HEADS = D_MODEL // HEAD_DIM
DECAY_LORA = 64
A_LORA = 64
GATE_LORA = 128
D_FF = 2816
N_MIXERS = 2
N_RWKV_LAYERS = (DEPTH + 1) // 2
N_POOL_LAYERS = DEPTH // 2
POOL_WINDOWS = (2, 4, 8, 16)
N_POOL_GROUPS = 4
POOL_GC = D_MODEL // N_POOL_GROUPS
N_MOD = 9
RMS_EPS = 1e-6
GN_EPS = 64e-5

kernel_name = 'hybrid_rwkv7_pool_diffusion_step'


def rms_norm(x, g):
    xf = x.astype(jnp.float32)
    y = xf * lax.rsqrt(jnp.mean(xf * xf, axis=-1, keepdims=True) + RMS_EPS)
    return (y * g.astype(jnp.float32)).astype(x.dtype)


def premod(x, g_pre, shift, scale):
    return rms_norm(x, g_pre) * (1 + scale) + shift


def swiglu(h, w1, w3, w2):
    return (jax.nn.silu(h @ w1) * (h @ w3)) @ w2


def wkv_scan(S0, r, w, k, v, kk, a, reverse):
    seq = tuple(jnp.moveaxis(t.astype(jnp.float32), 1, 0) for t in (r, w, k, v, kk, a))

    def step(S, inp):
        r_t, w_t, k_t, v_t, kk_t, a_t = inp
        sa = jnp.einsum('bhij,bhj->bhi', S, -kk_t)
        S = (S * w_t[:, :, None, :]
             + sa[..., None] * (kk_t * a_t)[:, :, None, :]
             + v_t[..., None] * k_t[:, :, None, :])
        return S, jnp.einsum('bhij,bhj->bhi', S, r_t)

    S, ys = lax.scan(step, S0.astype(jnp.float32), seq, reverse=reverse)
    return jnp.moveaxis(ys, 0, 1), S


def head_group_norm(y, w, b):
    yf = y.astype(jnp.float32)
    mean = jnp.mean(yf, axis=-1, keepdims=True)
    var = jnp.mean(jnp.square(yf - mean), axis=-1, keepdims=True)
    yn = (yf - mean) * lax.rsqrt(var + GN_EPS)
    return yn.reshape(y.shape[0], y.shape[1], -1) * w + b


def rwkv_time_mix(h, S0f, S0b, mu, w_rkv, w0, w1, w2, a0, a1, a2, g1, g2, k_k, k_a, r_k, gn_w, gn_b, w_o):
    B, T, D = h.shape
    heads = lambda t: t.reshape(B, T, N_HEADS, HEAD_DIM)
    hp = jnp.pad(h, ((0, 0), (1, 1), (0, 0)))
    xx = 0.5 * (hp[:, :-2] + hp[:, 2:]) - h
    xs = h[None] + xx[None] * mu[:, None, None, :]
    xr, xw, xk, xv, xa, xg = xs[0], xs[1], xs[2], xs[3], xs[4], xs[5]
    rkv = jnp.einsum('nbtd,nde->nbte', jnp.stack([xr, xk, xv]), w_rkv)
    r, k, v = heads(rkv[0]), rkv[1], heads(rkv[2])
    g = jax.nn.sigmoid(xg @ g1) @ g2
    kk = heads(k * k_k).astype(jnp.float32)
    kk = kk / jnp.maximum(jnp.sqrt(jnp.sum(kk * kk, axis=-1, keepdims=True)), 1e-12)
    ys, bonuses, finals = [], [], []
    for d, S0 in enumerate((S0f, S0b)):
        wlog = -jax.nn.softplus(-(w0[d] + jnp.tanh(xw @ w1[d]) @ w2[d])) - 0.5
        decay = jnp.exp(-jnp.exp(wlog.astype(jnp.float32)))
        a = jax.nn.sigmoid(a0[d] + (xa @ a1[d]) @ a2[d])
        kd = heads(k * (1 + (a - 1) * k_a))
        y_d, S_d = wkv_scan(S0, r, heads(decay), kd, v, kk, heads(a), reverse=(d == 1))
        ys.append(y_d)
        bonuses.append(jnp.sum(r * kd * r_k, axis=-1, keepdims=True) * v)
        finals.append(S_d)
    y = head_group_norm(ys[0] + ys[1], gn_w, gn_b) + (bonuses[0] + bonuses[1]).reshape(B, T, D)
    out = (y.astype(h.dtype) * g) @ w_o
    return out, finals[0], finals[1]


def centred_box_mean(x, window, axis):
    L = x.shape[axis]
    cs = jnp.cumsum(x.astype(jnp.float32), axis=axis)
    pad_cfg = [(0, 0)] * x.ndim
    pad_cfg[axis] = (1, 0)
    cs = jnp.pad(cs, pad_cfg)
    t = jnp.arange(L)
    lo = jnp.clip(t - window // 2, 0, L)
    hi = jnp.clip(t + window - window // 2, 0, L)
    s = jnp.take(cs, hi, axis=axis) - jnp.take(cs, lo, axis=axis)
    cshape = [1] * x.ndim
    cshape[axis] = L
    cnt = (hi - lo).astype(jnp.float32).reshape(cshape)
    return (s / cnt).astype(x.dtype)


def pool_mix(h, pool_w, pool_scale, grid):
    B, T, D = h.shape
    if grid:
        rows = T // GRID_W
        hg = h.reshape(B, rows, GRID_W, D)
        axes = (1, 2)
    else:
        hg = h
        axes = (1,)
    outs = []
    for gi, win in enumerate(POOL_WINDOWS):
        part = hg[..., gi * POOL_GC:(gi + 1) * POOL_GC]
        m = part
        for ax in axes:
            m = centred_box_mean(m, win, ax)
        outs.append(m - part)
    dlt = jnp.stack(outs, axis=-2)
    y = jnp.einsum('...gc,gce->...ge', dlt, pool_w)
    return y.reshape(B, T, D) * pool_scale


def trunk(x, cond, S0f_all, S0b_all, grid, W):
    B = x.shape[0]
    finals_f, finals_b = [], []
    for i in range(DEPTH):
        mod = jax.nn.silu(cond) @ W['w_mod'][i] + W['b_mod'][i]
        m = jnp.split(mod[:, None, :], N_MOD, axis=-1)
        h = premod(x, W['norm_pre'][i, 0], m[0], m[1])
        f = swiglu(h, W['ffn_w1'][i, 0], W['ffn_w3'][i, 0], W['ffn_w2'][i, 0])
        x = x + 0.5 * m[2] * rms_norm(f, W['norm_post'][i, 0])
        h = premod(x, W['norm_pre'][i, 1], m[3], m[4])
        j = i // N_MIXERS
        if i % N_MIXERS == 0:
            if S0f_all is None:
                s0f = jnp.zeros((B, N_HEADS, HEAD_DIM, HEAD_DIM), jnp.float32)
                s0b = s0f
            else:
                s0f, s0b = S0f_all[:, j], S0b_all[:, j]
            out, sf, sb = rwkv_time_mix(
                h, s0f, s0b, W['rwkv_mu'][j], W['rwkv_w_rkv'][j], W['rwkv_w0'][j], W['rwkv_w1'][j],
                W['rwkv_w2'][j], W['rwkv_a0'][j], W['rwkv_a1'][j], W['rwkv_a2'][j], W['rwkv_g1'][j],
                W['rwkv_g2'][j], W['rwkv_k_k'][j], W['rwkv_k_a'][j], W['rwkv_r_k'][j], W['rwkv_gn_w'][j],
                W['rwkv_gn_b'][j], W['rwkv_w_o'][j])
            finals_f.append(sf.astype(x.dtype))
            finals_b.append(sb.astype(x.dtype))
        else:
            out = pool_mix(h, W['pool_w'][j], W['pool_scale'][j], grid)
        x = x + m[5] * rms_norm(out, W['norm_post'][i, 1])
        h = premod(x, W['norm_pre'][i, 2], m[6], m[7])
        f = swiglu(h, W['ffn_w1'][i, 1], W['ffn_w3'][i, 1], W['ffn_w2'][i, 1])
        x = x + 0.5 * m[8] * rms_norm(f, W['norm_post'][i, 2])
    return x, jnp.stack(finals_f, axis=1), jnp.stack(finals_b, axis=1)


def setup_inputs(seed: int = 0) -> dict:
    key = jax.random.key(seed)
    ks = jax.random.split(key, 40)
    D = D_MODEL
    NR, NP = N_RWKV_LAYERS, N_POOL_LAYERS

    def nrm(k, shape, scale):
        return jax.random.normal(k, shape, jnp.float32) * scale

    return {
        'x_prompt': nrm(ks[0], (BATCH, SEQ, D), 1.0),
        'x_sample': nrm(ks[1], (DEC_BATCH, DEC_SEQ, D), 1.0),
        'c': nrm(ks[2], (DEC_BATCH, D), 1.0),
        'state_ctx_fwd': nrm(ks[3], (DEC_BATCH, NR, N_HEADS, HEAD_DIM, HEAD_DIM), 0.5),
        'state_ctx_bwd': nrm(ks[4], (DEC_BATCH, NR, N_HEADS, HEAD_DIM, HEAD_DIM), 0.5),
        'c_ctx': nrm(ks[5], (D,), 1.0),
        'w_mod': nrm(ks[6], (DEPTH, D, N_MOD * D), 0.5 * D ** -0.5),
        'b_mod': nrm(ks[7], (DEPTH, N_MOD * D), 0.02),
        'norm_pre': 1.0 + nrm(ks[8], (DEPTH, 3, D), 0.02),
        'norm_post': 1.0 + nrm(ks[9], (DEPTH, 3, D), 0.02),
        'ffn_w1': nrm(ks[10], (DEPTH, 2, D, D_FF), D ** -0.5),
        'ffn_w3': nrm(ks[11], (DEPTH, 2, D, D_FF), D ** -0.5),
        'ffn_w2': nrm(ks[12], (DEPTH, 2, D_FF, D), D_FF ** -0.5),
        'rwkv_mu': jax.random.uniform(ks[13], (NR, 6, D), jnp.float32),
        'rwkv_w_rkv': nrm(ks[14], (NR, 3, D, D), D ** -0.5),
        'rwkv_w0': jnp.linspace(-6.0, -1.0, D, dtype=jnp.float32)[None, None, :] + nrm(ks[15], (NR, 2, D), 0.1),
        'rwkv_w1': nrm(ks[16], (NR, 2, D, DECAY_LORA), D ** -0.5),
        'rwkv_w2': nrm(ks[17], (NR, 2, DECAY_LORA, D), 0.1 * DECAY_LORA ** -0.5),
        'rwkv_a0': nrm(ks[18], (NR, 2, D), 0.1),
        'rwkv_a1': nrm(ks[19], (NR, 2, D, A_LORA), D ** -0.5),
        'rwkv_a2': nrm(ks[20], (NR, 2, A_LORA, D), A_LORA ** -0.5),
        'rwkv_g1': nrm(ks[21], (NR, D, GATE_LORA), D ** -0.5),
        'rwkv_g2': nrm(ks[22], (NR, GATE_LORA, D), GATE_LORA ** -0.5),
        'rwkv_k_k': 0.85 + nrm(ks[23], (NR, D), 0.02),
        'rwkv_k_a': 1.0 + nrm(ks[24], (NR, D), 0.02),
        'rwkv_r_k': nrm(ks[25], (NR, N_HEADS, HEAD_DIM), 0.1),
        'rwkv_gn_w': 1.0 + nrm(ks[26], (NR, D), 0.02),
        'rwkv_gn_b': nrm(ks[27], (NR, D), 0.02),
        'rwkv_w_o': nrm(ks[28], (NR, D, D), D ** -0.5),
        'pool_w': nrm(ks[29], (NP, N_POOL_GROUPS, POOL_GC, POOL_GC), POOL_GC ** -0.5),
        'pool_scale': 1.0 + nrm(ks[30], (NP, D), 0.1),
    }


def reference(x_prompt, x_sample, c, state_ctx_fwd, state_ctx_bwd, c_ctx, w_mod, b_mod, norm_pre, norm_post,
              ffn_w1, ffn_w3, ffn_w2, rwkv_mu, rwkv_w_rkv, rwkv_w0, rwkv_w1, rwkv_w2, rwkv_a0, rwkv_a1,
              rwkv_a2, rwkv_g1, rwkv_g2, rwkv_k_k, rwkv_k_a, rwkv_r_k, rwkv_gn_w, rwkv_gn_b, rwkv_w_o,
              pool_w, pool_scale):
    W = dict(w_mod=w_mod, b_mod=b_mod, norm_pre=norm_pre, norm_post=norm_post,
             ffn_w1=ffn_w1, ffn_w3=ffn_w3, ffn_w2=ffn_w2,
             rwkv_mu=rwkv_mu, rwkv_w_rkv=rwkv_w_rkv, rwkv_w0=rwkv_w0, rwkv_w1=rwkv_w1, rwkv_w2=rwkv_w2,
             rwkv_a0=rwkv_a0, rwkv_a1=rwkv_a1, rwkv_a2=rwkv_a2, rwkv_g1=rwkv_g1, rwkv_g2=rwkv_g2,
             rwkv_k_k=rwkv_k_k, rwkv_k_a=rwkv_k_a, rwkv_r_k=rwkv_r_k, rwkv_gn_w=rwkv_gn_w,
             rwkv_gn_b=rwkv_gn_b, rwkv_w_o=rwkv_w_o, pool_w=pool_w, pool_scale=pool_scale)
    y_prompt, new_state_ctx_fwd, new_state_ctx_bwd = trunk(x_prompt, c_ctx[None, :], None, None, False, W)
    y_sample, _, _ = trunk(x_sample, c, state_ctx_fwd, state_ctx_bwd, True, W)
    return (y_prompt, y_sample, new_state_ctx_fwd, new_state_ctx_bwd)
```

```python
import contextlib
import os
import numpy as np
import concourse.bass as bass
import concourse.mybir as mybir
from concourse.bass_utils import run_bass_kernel_spmd

F32 = mybir.dt.float32
BF16 = mybir.dt.bfloat16
AF = mybir.ActivationFunctionType
ALU = mybir.AluOpType

D = 1024
KC = 8
DFF = 2816
FC = 22
TS = 2048
TP = 1024
TTOT = TS + TP
RMS_EPS = 1e-6
GN_EPS = 64e-5
POOL_WINDOWS = (2, 4, 8, 16)
ENGINES = ("pe", "act", "dve", "pool", "sp")
SAME_ENG_DIST = 10 ** 9


class V:
    __slots__ = ("t", "ap")

    def __init__(self, t, ap):
        self.t = t
        self.ap = ap

    def __getitem__(self, k):
        return V(self.t, self.ap[k])

    def bc(self, shape):
        return V(self.t, self.ap.to_broadcast(list(shape)))

    def re(self, pat, **kw):
        return V(self.t, self.ap.rearrange(pat, **kw))

    def bit(self, dt):
        return V(self.t, self.ap.bitcast(dt))


class T:
    __slots__ = ("ap", "name", "w", "r", "sem", "dcount", "frozen")

    def __init__(self, ap, name=""):
        if ap is not None and not isinstance(ap, bass.AP):
            ap = ap[:]
        self.ap = ap
        self.name = name
        self.w = None
        self.r = []
        self.sem = None
        self.dcount = 0
        self.frozen = False

    def __getitem__(self, k):
        return V(self, self.ap[k])

    @property
    def v(self):
        return V(self, self.ap)


class Op:
    __slots__ = ("eng", "fn", "deps", "dma", "lane", "lane_val", "signaled", "sigval", "pos")

    def __init__(self, eng, fn, dma=False):
        self.eng = eng
        self.fn = fn
        self.deps = []
        self.dma = dma
        self.lane = None
        self.lane_val = 0
        self.signaled = False
        self.sigval = 0


class Prog:
    SEM_LIMIT = 30000

    def __init__(self, nc, stack):
        self.nc = nc
        self.stack = stack
        self.ops = {e: [] for e in ENGINES}
        self.last = {e: None for e in ENGINES}
        self.lane_last = {}
        self.bar_deps = []
        self.final_waits = []
        self.same_engine_sync = {"pe": False, "act": True, "dve": True, "pool": True, "sp": False}
        self.nops = 0
        self.sem_pool = {}
        self.nsem = 0

    def new_sem(self, name):
        self.nsem += 1
        return self.stack.enter_context(self.nc.semaphore(name))

    def seal(self, ops):
        for op in ops:
            op.lane_val = op.lane.dcount

    def sbuf(self, name, shape, dt):
        return self.stack.enter_context(self.nc.sbuf_tensor("sb_" + name, list(shape), dt))

    def psum(self, name, shape, dt=F32):
        return self.stack.enter_context(self.nc.psum_tensor("pp_" + name, list(shape), dt))

    def add(self, eng, fn, reads=(), writes=(), dma=False, lane_tile=None):
        op = Op(eng, fn, dma)
        deps = []
        for t in reads:
            if t.w is not None:
                deps.append(t.w)
        for t in writes:
            if t.w is not None:
                deps.append(t.w)
            deps.extend(t.r)
        op.deps = deps
        for d in deps:
            if not d.dma:
                d.signaled = True
        if dma:
            lt = lane_tile
            if lt.sem is None:
                if lt.name in self.sem_pool:
                    lt.sem, lt.dcount = self.sem_pool[lt.name]
                else:
                    lt.sem = self.new_sem("d_" + lt.name)
            lt.dcount += 16
            self.sem_pool[lt.name] = (lt.sem, lt.dcount)
            op.lane = lt
            op.lane_val = lt.dcount
            self.lane_last[id(lt)] = op
        else:
            self.last[eng] = op
        for t in reads:
            if not t.frozen:
                t.r.append(op)
        for t in writes:
            t.w = op
            t.r = []
        self.ops[eng].append(op)
        self.nops += 1
        return op

    def barrier(self, extra_tiles=()):
        deps = []
        for e in ENGINES:
            if self.last[e] is not None:
                self.last[e].signaled = True
                deps.append(self.last[e])
        deps.extend(self.lane_last.values())
        self.lane_last = {}
        self.bar_deps = deps
        for t in extra_tiles:
            t.r.extend(deps)

    def finish(self, block):
        engsems = {}
        L = self.SEM_LIMIT
        for e in ENGINES:
            c = 0
            for op in self.ops[e]:
                if op.dma:
                    continue
                if op.signaled:
                    c += 1
                    op.sigval = c
            nsem = max(1, (c + L - 1) // L)
            engsems[e] = [self.new_sem(f"e_{e}{i}") for i in range(nsem)]
        self.engsems = engsems
        same = self.same_engine_sync
        final_waits = self.final_waits

        for e in ENGINES:
            c = 0
            for op in self.ops[e]:
                if not op.dma:
                    c += 1
                op.pos = c

        def emit_engine(e, eng):
            known = {}
            for op in self.ops[e]:
                need = {}
                for d in op.deps:
                    if d.dma:
                        key = ("l", id(d.lane))
                        val = d.lane_val
                        sem = d.lane.sem
                    else:
                        if d.eng == e and not same[e]:
                            continue
                        if d.eng == e and (not op.dma) and (op.pos - d.pos >= SAME_ENG_DIST):
                            continue
                        si = (d.sigval - 1) // L
                        key = ("e", d.eng, si)
                        val = d.sigval - si * L
                        sem = engsems[d.eng][si]
                    if known.get(key, 0) >= val:
                        continue
                    if key not in need or need[key][1] < val:
                        need[key] = (sem, val)
                for key, (sem, val) in need.items():
                    eng.wait_ge(sem, val)
                    known[key] = val
                ins = op.fn(eng)
                if op.dma:
                    ins.then_inc(op.lane.sem, 16)
                elif op.signaled:
                    si = (op.sigval - 1) // L
                    ins.then_inc(engsems[e][si], 1)
            if e == "sp":
                for (sem, val) in final_waits:
                    eng.wait_ge(sem, val)

        @block.tensor
        def _(eng):
            emit_engine("pe", eng)

        @block.scalar
        def _(eng):
            emit_engine("act", eng)

        @block.vector
        def _(eng):
            emit_engine("dve", eng)

        @block.gpsimd
        def _(eng):
            emit_engine("pool", eng)

        @block.sync
        def _(eng):
            emit_engine("sp", eng)

    def wait_final(self, tile):
        if tile.sem is not None:
            self.final_waits.append((tile.sem, tile.dcount))


def _tl(*vs):
    return [x.t for x in vs if isinstance(x, V)]


def _a(x):
    return x.ap if isinstance(x, V) else x


class Arena:
    def __init__(self, P, nbytes):
        self.P = P
        self.nbytes = nbytes
        self.raw = P.sbuf("arena", [128, nbytes // 2], BF16)
        self.ptr = 0
        self.hi = 0
        self.where = {}

    def reset(self, base):
        self.ptr = base

    def alloc(self, name, shape, dt):
        n = 1
        for s in shape:
            n *= s
        sz = n * (4 if dt == F32 else 2)
        sz = (sz + 63) // 64 * 64
        off = self.ptr
        assert off + sz <= self.nbytes, f"arena overflow at {name}: {off}+{sz} > {self.nbytes}"
        self.ptr += sz
        self.hi = max(self.hi, self.ptr)
        a = self.raw[:, off // 2: off // 2 + (n * (2 if dt == F32 else 1))]
        if dt == F32:
            a = a.bitcast(F32)
        if len(shape) == 2:
            a = a.rearrange("p (a b) -> p a b", a=shape[0])
        elif len(shape) == 3:
            a = a.rearrange("p (a b c) -> p a b c", a=shape[0], b=shape[1])
        t = T(a, name)
        t.r = list(self.P.bar_deps)
        self.where[name] = (off, tuple(shape), dt)
        return t


VEC_LAYOUT = [
    ("npre", 48), ("npost", 48), ("bmod", 144), ("mu", 48), ("w0", 16), ("a0", 16),
    ("k_k", 8), ("k_a", 8), ("r_k", 8), ("gn_w", 8), ("gn_b", 8), ("pscale", 8), ("cond", 16),
    ("eps", 8),
]
VEC_OFF = {}
_o = 0
for _n, _w in VEC_LAYOUT:
    VEC_OFF[_n] = _o
    _o += _w
NV = _o
CST_N = 21 * 128
ARENA_BYTES = 198 * 1024
XBYTES = KC * TS * 4


class KB:
    def __init__(self, stop=None):
        self.stop = stop
        self.nc = bass.Bass("TRN2", target_bir_lowering=False)
        self.stack = contextlib.ExitStack()

    def mm(self, out, lhsT, rhs, start=True, stop=True):
        o, l, r = out.ap, lhsT.ap, rhs.ap
        self.P.add("pe", lambda e: e.matmul(o, lhsT=l, rhs=r, start=start, stop=stop),
                   [lhsT.t, rhs.t], [out.t])

    def tr(self, out, in_, ident):
        o, i, d = out.ap, in_.ap, ident.ap
        self.P.add("pe", lambda e: e.transpose(o, i, d), [in_.t, ident.t], [out.t])

    def act(self, out, in_, func, bias=None, scale=1.0):
        o, i, b, s = out.ap, in_.ap, _a(bias), _a(scale)
        reads = _tl(in_, bias, scale)
        if bias is None:
            self.P.add("act", lambda e: e.activation(out=o, in_=i, func=func, scale=s), reads, [out.t])
        else:
            self.P.add("act", lambda e: e.activation(out=o, in_=i, func=func, bias=b, scale=s), reads, [out.t])

    def tt(self, eng, out, a, b, op):
        o, x, y = out.ap, a.ap, b.ap
        self.P.add(eng, lambda e: e.tensor_tensor(out=o, in0=x, in1=y, op=op), [a.t, b.t], [out.t])

    def ts(self, eng, out, a, s1, op0, s2=None, op1=None):
        o, x, p1, p2 = out.ap, a.ap, _a(s1), _a(s2)
        reads = _tl(a, s1, s2)
        if op1 is None:
            self.P.add(eng, lambda e: e.tensor_single_scalar(out=o, in_=x, scalar=p1, op=op0), reads, [out.t])
        else:
            self.P.add(eng, lambda e: e.tensor_scalar(out=o, in0=x, scalar1=p1, scalar2=p2, op0=op0, op1=op1),
                       reads, [out.t])

    def stt(self, out, a, scalar, b, op0, op1):
        o, x, s, y = out.ap, a.ap, _a(scalar), b.ap
        self.P.add("dve", lambda e: e.scalar_tensor_tensor(out=o, in0=x, scalar=s, in1=y, op0=op0, op1=op1),
                   _tl(a, scalar, b), [out.t])

    def cp(self, eng, out, in_):
        o, i = out.ap, in_.ap
        if eng == "act":
            self.P.add("act", lambda e: e.activation(out=o, in_=i, func=AF.Copy), [in_.t], [out.t])
        else:
            self.P.add(eng, lambda e: e.tensor_copy(out=o, in_=i), [in_.t], [out.t])

    def recip(self, out, in_):
        o, i = out.ap, in_.ap
        self.P.add("dve", lambda e: e.reciprocal(out=o, in_=i), [in_.t], [out.t])

    def memset(self, eng, out, val):
        o = out.ap
        self.P.add(eng, lambda e: e.memset(o, val), [], [out.t])

    def cumsum(self, out, ones, data):
        o, a, b = out.ap, ones.ap, data.ap
        self.P.add("dve", lambda e: e.tensor_tensor_scan(out=o, data0=a, data1=b, initial=0.0,
                                                        op0=ALU.mult, op1=ALU.add),
                   [ones.t, data.t], [out.t])

    def dma_in(self, eng, out, dram_ap, lane=None):
        o = out.ap
        return self.P.add(eng, lambda e: e.dma_start(out=o, in_=dram_ap), [], [out.t], dma=True,
                          lane_tile=(lane if lane is not None else out.t))

    def dma_out(self, eng, dram_ap, in_, lane=None):
        i = in_.ap
        return self.P.add(eng, lambda e: e.dma_start(out=dram_ap, in_=i), [in_.t], [], dma=True,
                          lane_tile=(lane if lane is not None else in_.t))

    def build(self):
        nc = self.nc
        st = self.stack
        with st:
            self.P = P = Prog(nc, st)
            di = lambda name, shape: nc.dram_tensor(name, list(shape), F32, kind="ExternalInput").ap()
            do = lambda name, shape: nc.dram_tensor(name, list(shape), F32, kind="ExternalOutput").ap()
            self.d_xT = di("xT", [D, TTOT])
            self.d_vecs = di("vecs", [128, NV])
            self.d_cst = di("cst", [128, CST_N])
            self.d_pcst = di("pcst", [128, 4 * (256 + 32 + 64)])
            self.d_s0 = di("s0", [2, 8, 128, 64])
            self.d_wmod = di("w_mod", [2, D, 9 * D])
            self.d_w1 = di("ffn_w1", [2, 2, D, DFF])
            self.d_w3 = di("ffn_w3", [2, 2, D, DFF])
            self.d_w2 = di("ffn_w2", [2, 2, DFF, D])
            self.d_wrkv = di("rwkv_w_rkv", [3, D, D])
            self.d_lw1 = di("rwkv_w1", [2, D, 64])
            self.d_lw2 = di("rwkv_w2", [2, 64, D])
            self.d_la1 = di("rwkv_a1", [2, D, 64])
            self.d_la2 = di("rwkv_a2", [2, 64, D])
            self.d_g1 = di("rwkv_g1", [D, 128])
            self.d_g2 = di("rwkv_g2", [128, D])
            self.d_wo = di("rwkv_w_o", [D, D])
            self.d_pw = di("pool_w", [4, 256, 256])
            self.d_yT = do("yT", [D, TTOT])
            self.d_ns = do("ns", [2, 4, 8, 128, 64])

            self.vecs = T(P.sbuf("vecs", [128, NV], F32), "vecs")
            self.cstb = T(P.sbuf("cstb", [128, CST_N], BF16), "cstb")
            self.bonesf_t = T(P.sbuf("bonesf", [128, 128], F32), "bonesf")
            self.maskn = T(P.sbuf("maskn", [128, 2, 128], BF16), "maskn")
            self.modv = T(P.sbuf("modv", [128, 288], F32), "modv")
            self.gsv = T(P.sbuf("gsv", [128, 96], F32), "gsv")
            self.gpv = T(P.sbuf("gpv", [128, 96], F32), "gpv")
            self.scb = T(P.sbuf("scb", [128, 16], BF16), "scb")
            self.ps = [T(P.psum(f"ps{i}", [128, 512], F32), f"ps{i}") for i in range(8)]
            self.modtmp = T(P.sbuf("modtmp", [128, 16], F32), "modtmp")
            self.A = Arena(P, ARENA_BYTES)
            X = self.A.raw[:, 0:XBYTES // 2].bitcast(F32).rearrange("p (c t) -> p c t", c=KC)
            self.Xap = X
            self.XT = [[T(X[:, c, tt * 512:(tt + 1) * 512], f"X{c}_{tt}") for tt in range(4)] for c in range(KC)]

            self.lane_x = [T(None, f"lx{i}") for i in range(4)]
            self.lane_y = T(None, "ly")
            self.lane_ns = [T(None, "lns0"), T(None, "lns1")]
            self.setup()
            phases = [
                dict(name="s", tok0=0, T=TS, q=0, grid=True, seqs=[(0, TS)]),
                dict(name="p", tok0=TS, T=TP, q=1, grid=False, seqs=[(i * 256, 256) for i in range(4)]),
            ]
            for ph in phases:
                self.run_phase(ph)
            with nc.Block() as block:
                P.finish(block)
        return nc

    def vcol(self, name, idx, n=1):
        o = VEC_OFF[name] + idx
        return self.vecs[:, o:o + n]

    def setup(self):
        P = self.P
        self.dma_in("sp", self.vecs.v, self.d_vecs[:, :])
        self.dma_in("pool", self.cstb.v, self.d_cst[:, :])
        self.dma_in("sp", self.bonesf_t.v, self.d_cst[:, 256:384])
        self.vecs.frozen = True
        for m in range(2):
            self.ts("dve", self.maskn[:, m, :], self.cstb[:, (5 + m) * 128:(6 + m) * 128], -1.0, ALU.mult)
        self.cstb.frozen = True
        self.bonesf_t.frozen = True
        self.maskn.frozen = True
        self.ident = self.cstb[:, 0:128]
        self.ones = self.cstb[:, 128:256]
        self.bones = self.cstb[:, 256:384]
        self.bonesf = self.bonesf_t.v
        self.mSU = self.cstb[:, 384:512]
        self.mSL = self.cstb[:, 512:640]
        self.mUI = self.cstb[:, 640:768]
        self.mLI = self.cstb[:, 768:896]
        self.mnUI = self.maskn[:, 0, :]
        self.mnLI = self.maskn[:, 1, :]
        self.patL = [self.cstb[:, (7 + i) * 128:(8 + i) * 128] for i in range(7)]
        self.patU = [self.cstb[:, (14 + i) * 128:(15 + i) * 128] for i in range(7)]
        self.act(self.scb.v, self.vcol("cond", 0, 16), AF.Silu)
        A = self.A
        A.reset(XBYTES)
        P.barrier()
        wslots = [A.alloc(f"wm{i}", [8, 1024], BF16) for i in range(2)]
        psM = self.ps[7]
        cnt = 0
        for l in range(2):
            wm = self.d_wmod[l].rearrange("(kc p) n -> p kc n", p=128)
            for blk in range(9):
                ws = wslots[cnt % 2]
                cnt += 1
                self.dma_in("pool", ws.v, wm[:, :, blk * 1024:(blk + 1) * 1024])
                for jj in range(8):
                    j = blk * 8 + jj
                    col = (l * 72 + j) * 2
                    for kc in range(8):
                        self.mm(psM[:, col:col + 2], ws[:, kc, jj * 128:(jj + 1) * 128],
                                self.scb[:, kc * 2:kc * 2 + 2], start=(kc == 0), stop=(kc == 7))
        for l in range(2):
            bm = self.vcol("bmod", l * 72, 72)
            self.tt("dve", self.modv[:, l * 144:(l + 1) * 144].re("p (j q) -> p j q", q=2),
                    psM[:, l * 144:(l + 1) * 144].re("p (j q) -> p j q", q=2),
                    bm.re("p (j o) -> p j o", o=1).bc([128, 72, 2]), ALU.add)
        tmp = self.modtmp
        for l in range(2):
            for s in range(3):
                base = l * 144 + (3 * s) * 16
                o = (l * 3 + s) * 16
                npre = self.vcol("npre", (l * 3 + s) * 8, 8).re("p (c o) -> p c o", o=1).bc([128, 8, 2])
                npost = self.vcol("npost", (l * 3 + s) * 8, 8).re("p (c o) -> p c o", o=1).bc([128, 8, 2])
                self.ts("dve", tmp[:, :], self.modv[:, base + 16:base + 32], 1.0, ALU.add)
                self.tt("dve", self.gsv[:, o:o + 16].re("p (c q) -> p c q", q=2),
                        tmp[:, :].re("p (c q) -> p c q", q=2), npre, ALU.mult)
                self.tt("dve", tmp[:, :].re("p (c q) -> p c q", q=2),
                        self.modv[:, base + 32:base + 48].re("p (c q) -> p c q", q=2), npost, ALU.mult)
                self.ts("dve", self.gpv[:, o:o + 16], tmp[:, :], 0.5 if s != 1 else 1.0, ALU.mult)
        self.modv.frozen = True
        self.gsv.frozen = True
        self.gpv.frozen = True

    def gs(self, l, s, c, q):
        o = (l * 3 + s) * 16 + c * 2 + q
        return self.gsv[:, o:o + 1]

    def gp(self, l, s, c, q):
        o = (l * 3 + s) * 16 + c * 2 + q
        return self.gpv[:, o:o + 1]

    def shift(self, l, s, c, q):
        o = l * 144 + (3 * s) * 16 + c * 2 + q
        return self.modv[:, o:o + 1]

    def run_phase(self, ph):
        P = self.P
        Tn = ph["T"]
        ph["NT"] = Tn // 512
        xall = [t for c in range(KC) for t in self.XT[c]]
        P.barrier(extra_tiles=xall)
        for tt in range(ph["NT"]):
            ops = []
            for c in range(KC):
                ops.append(self.dma_in("sp", self.XT[c][tt].v,
                                       self.d_xT[c * 128:(c + 1) * 128,
                                                 ph["tok0"] + tt * 512: ph["tok0"] + (tt + 1) * 512],
                                       lane=self.lane_x[tt]))
            P.seal(ops)
        stages = [("ffn", 0, 0), ("rwkv", 0, 1), ("ffn", 0, 2), ("ffn", 1, 0), ("pool", 1, 1), ("ffn", 1, 2)]
        nst = len(stages) if self.stop is None else self.stop
        for (kind, l, s) in stages[:nst]:
            if kind == "ffn":
                self.ffn(ph, l, s)
            elif kind == "rwkv":
                self.rwkv(ph, l, s)
            else:
                self.pool(ph, l, s)
        ops = []
        for c in range(KC):
            for tt in range(ph["NT"]):
                ops.append(self.dma_out("sp", self.d_yT[c * 128:(c + 1) * 128,
                                                        ph["tok0"] + tt * 512: ph["tok0"] + (tt + 1) * 512],
                                        self.XT[c][tt].v, lane=self.lane_y))
        P.seal(ops)
        P.wait_final(self.lane_y)

    def rms_rstd(self, srcs, sqs, psq, rstd, n=512, eps_idx=0):
        for c in range(KC):
            self.act(sqs[c], srcs[c], AF.Square)
            self.mm(psq, self.ones, sqs[c], start=(c == 0), stop=(c == KC - 1))
        self.act(rstd, psq, AF.Sqrt, bias=self.vcol("eps", eps_idx), scale=1.0 / D)
        self.recip(rstd, rstd)

    def ffn(self, ph, l, s):
        P, A = self.P, self.A
        q = ph["q"]
        si = 0 if s == 0 else 1
        P.barrier()
        A.reset(XBYTES)
        h = [[A.alloc(f"h{c}_{t}", [512], BF16) for t in range(2)] for c in range(KC)]
        g = [[A.alloc(f"g{f}_{t}", [512], BF16) for t in range(2)] for f in range(FC)]
        w1s = [A.alloc(f"w1s{i}", [8, 256], BF16) for i in range(2)]
        w3s = [A.alloc(f"w3s{i}", [8, 256], BF16) for i in range(2)]
        w2s = [A.alloc(f"w2s{i}", [FC, 128], BF16) for i in range(2)]
        rstd = [A.alloc(f"rstd{i}", [512], F32) for i in range(2)]
        tmp = [A.alloc(f"tmp{i}", [512], F32) for i in range(2)]
        s1 = [A.alloc(f"s1_{i}", [512], BF16) for i in range(2)]
        fsb = [[A.alloc(f"fsb{c}_{t}", [512], F32) for t in range(2)] for c in range(KC)]
        w1d = self.d_w1[l, si].rearrange("(kc p) f -> p kc f", p=128)
        w3d = self.d_w3[l, si].rearrange("(kc p) f -> p kc f", p=128)
        w2d = self.d_w2[l, si].rearrange("(fc p) d -> p fc d", p=128)
        wcnt = 0
        w2cnt = 0
        pcnt = 0
        ocnt = 0
        tcnt = 0
        for grp in range(ph["T"] // 1024):
            for t in range(2):
                tt = grp * 2 + t
                srcs = [self.XT[c][tt].v for c in range(KC)]
                sqs = [fsb[c][t].v.bit(BF16)[:, 0:512] for c in range(KC)]
                rs = rstd[t]
                self.rms_rstd(srcs, sqs, self.ps[6].v, rs.v)
                for c in range(KC):
                    tm = tmp[tcnt % 2]
                    tcnt += 1
                    self.tt("dve", tm.v, srcs[c], rs.v, ALU.mult)
                    self.act(h[c][t].v, tm.v, AF.Identity, bias=self.shift(l, s, c, q), scale=self.gs(l, s, c, q))
            for fb in range(FC // 2):
                slot = wcnt % 2
                wcnt += 1
                self.dma_in("pool", w1s[slot].v, w1d[:, :, fb * 256:(fb + 1) * 256])
                self.dma_in("pool", w3s[slot].v, w3d[:, :, fb * 256:(fb + 1) * 256])
                for t in range(2):
                    for fi in range(2):
                        fc = fb * 2 + fi
                        pu1 = self.ps[(pcnt % 2) * 2]
                        pu3 = self.ps[(pcnt % 2) * 2 + 1]
                        sb = s1[pcnt % 2]
                        pcnt += 1
                        for kc in range(KC):
                            self.mm(pu1.v, w1s[slot][:, kc, fi * 128:(fi + 1) * 128], h[kc][t].v,
                                    start=(kc == 0), stop=(kc == KC - 1))
                        for kc in range(KC):
                            self.mm(pu3.v, w3s[slot][:, kc, fi * 128:(fi + 1) * 128], h[kc][t].v,
                                    start=(kc == 0), stop=(kc == KC - 1))
                        self.act(sb.v, pu1.v, AF.Silu)
                        self.tt("dve", g[fc][t].v, sb.v, pu3.v, ALU.mult)
            for dc in range(KC):
                slot = w2cnt % 2
                w2cnt += 1
                self.dma_in("pool", w2s[slot][:, 0:11, :], w2d[:, 0:11, dc * 128:(dc + 1) * 128])
                self.dma_in("pool", w2s[slot][:, 11:22, :], w2d[:, 11:22, dc * 128:(dc + 1) * 128])
                for t in range(2):
                    po = self.ps[4 + ocnt % 2]
                    ocnt += 1
                    for fc in range(FC):
                        self.mm(po.v, w2s[slot][:, fc, :], g[fc][t].v, start=(fc == 0), stop=(fc == FC - 1))
                    self.cp("dve", fsb[dc][t].v, po.v)
            for t in range(2):
                tt = grp * 2 + t
                srcs = [fsb[c][t].v for c in range(KC)]
                sqs = [h[c][t].v for c in range(KC)]
                rs = rstd[t]
                self.rms_rstd(srcs, sqs, self.ps[6].v, rs.v)
                for c in range(KC):
                    tm = tmp[tcnt % 2]
                    tcnt += 1
                    self.tt("dve", tm.v, srcs[c], rs.v, ALU.mult)
                    self.stt(self.XT[c][tt].v, tm.v, self.gp(l, s, c, q), self.XT[c][tt].v, ALU.mult, ALU.add)

    def pool(self, ph, l, s):
        P, A = self.P, self.A
        q = ph["q"]
        Tn = ph["T"]
        NT = ph["NT"]
        P.barrier()
        A.reset(XBYTES)
        if ph["grid"]:
            Ad, Bd, padA = 32, 64, 8
        else:
            Ad, Bd, padA = Tn // 256, 256, 0
        padB = 8
        RA, RB = Ad + 2 * padA, Bd + 2 * padB
        rpt = 512 // Bd
        rstd = A.alloc("p_rstd", [Tn], F32)
        D0 = A.alloc("p_d0", [RA, RB], F32)
        Pa = A.alloc("p_pa", [RA, RB], F32)
        Pb = A.alloc("p_pb", [RA, RB], F32)
        dlt = [A.alloc(f"p_dlt{i}", [2048], BF16) for i in range(2)]
        outs = [A.alloc(f"p_out{c}", [Tn], F32) for c in range(KC)]
        pws = [A.alloc(f"p_w{i}", [2, 256], BF16) for i in range(2)]
        pcw = 96 if ph["grid"] else 256
        pc = A.alloc("p_cst", [4, pcw], F32)
        tmp = [A.alloc(f"p_tmp{i}", [512], F32) for i in range(2)]
        sqs = [dlt[c // 4][:, (c % 4) * 512:(c % 4 + 1) * 512] for c in range(KC)]
        pcd = self.d_pcst.rearrange("p (w x) -> p w x", w=4)
        self.dma_in("sp", pc.v, pcd[:, :, 256:352] if ph["grid"] else pcd[:, :, 0:256])
        self.memset("dve", D0.v, 0.0)
        for tt in range(NT):
            srcs = [self.XT[c][tt].v for c in range(KC)]
            self.rms_rstd(srcs, sqs, self.ps[6].v, rstd[:, tt * 512:(tt + 1) * 512])
        tcnt = 0
        mcnt = 0
        for c in range(KC):
            gi = c // 2
            w = POOL_WINDOWS[gi]
            hw = w // 2
            L = w.bit_length() - 1
            eng = "dve" if c % 2 == 0 else "pool"
            for tt in range(NT):
                tm = tmp[tcnt % 2]
                tcnt += 1
                self.tt("dve", tm.v, self.XT[c][tt].v, rstd[:, tt * 512:(tt + 1) * 512], ALU.mult)
                self.act(D0[:, padA + tt * rpt: padA + (tt + 1) * rpt, padB:padB + Bd],
                         tm.v.re("p (a b) -> p a b", b=Bd), AF.Identity,
                         bias=self.shift(l, s, c, q), scale=self.gs(l, s, c, q))
            src = D0
            bufs = [Pa, Pb]
            bi = 0
            n = RB
            for m in range(L):
                sh = 1 << m
                n -= sh
                dst = bufs[bi]
                bi ^= 1
                self.tt(eng, dst[:, :, 0:n], src[:, :, 0:n], src[:, :, sh:sh + n], ALU.add)
                src = dst
            c0 = padB - hw
            if ph["grid"]:
                nr = RA
                for m in range(L):
                    sh = 1 << m
                    nr -= sh
                    dst = bufs[bi]
                    bi ^= 1
                    self.tt(eng, dst[:, 0:nr, c0:c0 + Bd], src[:, 0:nr, c0:c0 + Bd],
                            src[:, sh:sh + nr, c0:c0 + Bd], ALU.add)
                    src = dst
                res = src[:, padA - hw: padA - hw + Ad, c0:c0 + Bd]
            else:
                res = src[:, :, c0:c0 + Bd]
            oth = bufs[bi]
            o1 = oth[:, 0:Ad, 0:Bd]
            wi = gi
            if ph["grid"]:
                invA = pc[:, wi, 0:32].re("p (a o) -> p a o", o=1).bc([128, Ad, Bd])
                invB = pc[:, wi, 32:96].re("p (o b) -> p o b", o=1).bc([128, Ad, Bd])
                self.tt(eng, o1, res, invB, ALU.mult)
                self.tt(eng, o1, o1, invA, ALU.mult)
            else:
                invB = pc[:, wi, 0:256].re("p (o b) -> p o b", o=1).bc([128, Ad, Bd])
                self.tt(eng, o1, res, invB, ALU.mult)
            self.tt(eng, dlt[c % 2][:, 0:Tn].re("p (a b) -> p a b", b=Bd), o1,
                    D0[:, padA:padA + Ad, padB:padB + Bd], ALU.subtract)
            if c % 2 == 1:
                pw = pws[gi % 2]
                self.dma_in("pool", pw.v, self.d_pw[gi].rearrange("(kc p) e -> p kc e", p=128))
                for tt in range(NT):
                    for ec in range(2):
                        pb = self.ps[mcnt % 4]
                        mcnt += 1
                        for kc in range(2):
                            self.mm(pb.v, pw[:, kc, ec * 128:(ec + 1) * 128],
                                    dlt[kc][:, tt * 512:(tt + 1) * 512], start=(kc == 0), stop=(kc == 1))
                        oc = 2 * gi + ec
                        self.act(outs[oc][:, tt * 512:(tt + 1) * 512], pb.v, AF.Identity,
                                 scale=self.vcol("pscale", oc))
        for tt in range(NT):
            srcs = [outs[c][:, tt * 512:(tt + 1) * 512] for c in range(KC)]
            rs = rstd[:, tt * 512:(tt + 1) * 512]
            self.rms_rstd(srcs, sqs, self.ps[6].v, rs)
            for c in range(KC):
                tm = tmp[tcnt % 2]
                tcnt += 1
                self.tt("dve", tm.v, srcs[c], rs, ALU.mult)
                self.stt(self.XT[c][tt].v, tm.v, self.gp(l, s, c, q), self.XT[c][tt].v, ALU.mult, ALU.add)

    def nb(self):
        while True:
            i = self.bcnt % 8
            self.bcnt += 1
            if i not in self.nb_excl:
                return self.ps[i]

    def rwkv(self, ph, l, s):
        P, A = self.P, self.A
        self.bcnt = 0
        self.nb_excl = set()
        P.barrier()
        A.reset(XBYTES)
        sample = ph["grid"]
        R = {}
        self.R = R
        R["w2c"] = A.alloc("r_w2c", [1024], BF16)
        R["a2c"] = A.alloc("r_a2c", [1024], BF16)
        R["g2"] = A.alloc("r_g2", [1024], BF16)
        for d in range(2):
            self.dma_in("pool", R["w2c"][d * 64:(d + 1) * 64, :], self.d_lw2[d])
            self.dma_in("pool", R["a2c"][d * 64:(d + 1) * 64, :], self.d_la2[d])
        self.dma_in("pool", R["g2"].v, self.d_g2[:, :])
        R["omka"] = A.alloc("r_omka", [8], F32)
        self.ts("dve", R["omka"].v, self.vcol("k_a", 0, 8), -1.0, ALU.mult, 1.0, ALU.add)
        R["onesf"] = A.alloc("r_onesf", [512], F32)
        self.memset("dve", R["onesf"].v, 1.0)
        nsl = 1 if sample else 2
        R["M"] = [[[A.alloc(f"r_M{si}_{d}_{cc}", [64], F32) for cc in range(8)] for d in range(2)] for si in range(nsl)]
        if sample:
            R["xhs"] = A.alloc("r_xhs", [8, 8], F32)
            for c in range(KC):
                for qq in range(1, 4):
                    self.cp("dve", R["xhs"][:, c, 2 * (qq - 1):2 * (qq - 1) + 1], self.XT[c][qq - 1][:, 511:512])
                    self.cp("dve", R["xhs"][:, c, 2 * (qq - 1) + 1:2 * (qq - 1) + 2], self.XT[c][qq][:, 0:1])
            R["yfs"] = [[A.alloc(f"r_yf{qq}_{cc}", [512], BF16) for cc in range(8)] for qq in range(3)]
            for d in range(2):
                for cc in range(8):
                    self.dma_in("sp", R["M"][0][d][cc].v, self.d_s0[d, cc])
            passes = []
            for qq in range(3):
                passes.append(dict(p0=qq * 512, dirs=[0], chains={0: [([0, 1, 2, 3], "carry", None)]},
                                   haloL=(qq > 0), haloR=True, breaks=[], store=qq, load=None, final=False))
            passes.append(dict(p0=3 * 512, dirs=[0, 1],
                               chains={0: [([0, 1, 2, 3], "carry", None)], 1: [([3, 2, 1, 0], "carry", None)]},
                               haloL=True, haloR=False, breaks=[], store=None, load=None, final=True))
            for qq in (2, 1, 0):
                passes.append(dict(p0=qq * 512, dirs=[1], chains={1: [([3, 2, 1, 0], "carry", None)]},
                                   haloL=(qq > 0), haloR=True, breaks=[], store=None, load=qq, final=True))
        else:
            passes = []
            for pi in range(2):
                passes.append(dict(p0=pi * 512, dirs=[0, 1],
                                   chains={0: [([0, 1], "zero", 2 * pi), ([2, 3], "zero", 2 * pi + 1)],
                                           1: [([1, 0], "zero", 2 * pi), ([3, 2], "zero", 2 * pi + 1)]},
                                   haloL=False, haloR=False, breaks=[256], store=None, load=None, final=True))
        mark = A.ptr
        for pa in passes[:KPASS]:
            P.barrier()
            A.reset(mark)
            self.rwkv_pass(ph, l, s, pa)
        if sample:
            self.where_s = dict(A.where)
        if not sample:
            P.wait_final(self.lane_ns[0])
            P.wait_final(self.lane_ns[1])

    def rwkv_pass(self, ph, l, s, pa):
        P, A, R = self.P, self.A, self.R
        q = ph["q"]
        p0 = pa["p0"]
        tt = p0 // 512
        dirs = pa["dirs"]
        final = pa["final"]
        kdirs = [0, 1] if final else dirs
        ident = self.ident
        C_ED = 0.6065306597126334

        hx = [A.alloc(f"hx{c}", [1040], BF16) for c in range(KC)]
        hb = [hx[c][:, 0:514] for c in range(KC)]
        xx = [hx[c][:, 528:1040] for c in range(KC)]
        osb = [hx[c].v.bit(F32)[:, 0:512] for c in range(KC)]
        z = [A.alloc(f"z{c}", [512], BF16) for c in range(KC)] if final else None
        tw = A.alloc("tw", [512], BF16)
        ta = A.alloc("ta", [512], BF16)
        sg = A.alloc("sg", [512], BF16)
        wraw = [[A.alloc(f"wraw{sl}_{i}", [8, 128], BF16) for i in range(3)] for sl in range(2)]
        wsc1 = A.alloc("wsc0", [8, 128], BF16)
        wsc = [wsc1, wsc1, wsc1]
        f32t = [A.alloc(f"f32t{i}", [512], F32) for i in range(4)]
        rstd = f32t[3]
        xh = A.alloc("xh", [16], F32)
        xhb = A.alloc("xhb", [16], BF16)
        rstd2 = A.alloc("rstd2", [2], F32)
        r_sb = A.alloc("r_sb", [512], BF16)
        v_sb = A.alloc("v_sb", [512], BF16)
        kk = A.alloc("kk", [512], BF16)
        kf = A.alloc("kf", [512], F32)
        kd = [A.alloc(f"kd{d}", [512], BF16) for d in range(2)]
        bb = [A.alloc(f"bb{d}", [512], BF16) for d in range(2)]
        G = [A.alloc(f"G{d}", [520], F32) for d in range(2)]
        yacc = A.alloc("yacc", [512], F32)
        Ein1 = A.alloc("Ein", [512], F32)
        rW1 = A.alloc("rW", [512], BF16)
        kW1 = A.alloc("kW", [512], BF16)
        bW1 = A.alloc("bW", [512], BF16)
        kkW1 = A.alloc("kkW", [512], BF16)
        Ein, rW, kW, bW, kkW = [Ein1] * 2, [rW1] * 2, [kW1] * 2, [bW1] * 2, [kkW1] * 2
        fmt = [A.alloc(f"fmt{i}", [512], BF16) for i in range(2)]
        Vtok = A.alloc("Vtok", [4, 128], BF16)
        kWct = [A.alloc("kWct", [4, 128], BF16)] * 2
        nbWct = [A.alloc("nbWct", [4, 128], BF16)] * 2
        class H2:
            def __init__(h2, name):
                h2.h = [A.alloc(f"{name}_{i}", [4, 128], BF16) for i in range(2)]

            def u(h2, uu):
                return h2.h[uu // 4][:, uu % 4, :]
        TT = [H2("TT")] * 2
        AakT = [H2("AakT")] * 2
        ArkT = [H2("ArkT")] * 2
        nArbT = [H2("nArbT")] * 2
        tb = {k: H2("tb_" + k) for k in ("A0", "N0", "P0", "P1", "A1", "N1")}
        nsi = max(len(v_) for v_ in pa["chains"].values())
        Mb = [[A.alloc(f"Mb{si}_{i}", [64], BF16) for i in range(2)] for si in range(nsi)]
        Rb = [A.alloc(f"Rb{si}", [128], BF16) for si in range(nsi)]
        Ub = [A.alloc(f"Ub{si}", [128], BF16) for si in range(nsi)]
        for d in range(2):
            self.memset("dve", G[d][:, 0:1], 0.0)

        srcs = [self.XT[c][tt].v for c in range(KC)]
        sqs = [xx[c] for c in range(KC)]
        self.rms_rstd(srcs, sqs, self.nb().v, rstd.v)
        for c in range(KC):
            tm = f32t[c % 2]
            self.tt("dve", tm.v, srcs[c], rstd.v, ALU.mult)
            self.act(hb[c][:, 1:513], tm.v, AF.Identity, bias=self.shift(l, s, c, q), scale=self.gs(l, s, c, q))
        if pa["haloL"] or pa["haloR"]:
            self.memset("dve", xh.v, 1.0)
            for c in range(KC):
                if pa["haloL"]:
                    self.cp("dve", xh[:, 2 * c:2 * c + 1], R["xhs"][:, c, 2 * (tt - 1):2 * (tt - 1) + 1])
                if pa["haloR"]:
                    self.cp("dve", xh[:, 2 * c + 1:2 * c + 2], R["xhs"][:, c, 2 * tt + 1:2 * tt + 2])
            self.act(xhb.v, xh.v, AF.Square)
            pb = self.nb()
            for c in range(KC):
                self.mm(pb[:, 0:2], self.ones, xhb[:, 2 * c:2 * c + 2], start=(c == 0), stop=(c == KC - 1))
            self.act(rstd2.v, pb[:, 0:2], AF.Sqrt, bias=self.vcol("eps", 0), scale=1.0 / D)
            self.recip(rstd2.v, rstd2.v)
            for c in range(KC):
                self.tt("dve", xh[:, 2 * c:2 * c + 2], xh[:, 2 * c:2 * c + 2], rstd2.v, ALU.mult)
                if pa["haloL"]:
                    self.act(hb[c][:, 0:1], xh[:, 2 * c:2 * c + 1], AF.Identity,
                             bias=self.shift(l, s, c, q), scale=self.gs(l, s, c, q))
                if pa["haloR"]:
                    self.act(hb[c][:, 513:514], xh[:, 2 * c + 1:2 * c + 2], AF.Identity,
                             bias=self.shift(l, s, c, q), scale=self.gs(l, s, c, q))
        for c in range(KC):
            if not pa["haloL"]:
                self.memset("dve", hb[c][:, 0:1], 0.0)
            if not pa["haloR"]:
                self.memset("dve", hb[c][:, 513:514], 0.0)
            tm = f32t[2 + c % 2]
            self.tt("dve", tm.v, hb[c][:, 0:512], hb[c][:, 2:514], ALU.add)
            for bk in pa["breaks"]:
                self.cp("dve", tm[:, bk - 1:bk], hb[c][:, bk - 1:bk])
                self.cp("dve", tm[:, bk:bk + 1], hb[c][:, bk + 2:bk + 3])
            self.stt(xx[c], tm.v, 0.5, hb[c][:, 1:513], ALU.mult, ALU.subtract)

        if KCUT == 0:
            return
        mu_idx = {"r": 0, "w": 1, "k": 2, "v": 3, "a": 4, "g": 5}

        def scale_w(dst, src, mi):
            mu8 = self.vcol("mu", mi * 8, 8).re("p (k o) -> p k o", o=1).bc([128, 8, 128])
            self.tt("dve", dst.v, src.v, mu8, ALU.mult)

        def proj(pb, wr, ws, ncols=128):
            for kc in range(KC):
                self.mm(pb.v, wr[:, kc, 0:ncols], hb[kc][:, 1:513], start=(kc == 0), stop=False)
            for kc in range(KC):
                self.mm(pb.v, ws[:, kc, 0:ncols], xx[kc], start=False, stop=(kc == KC - 1))

        wl = wraw[0]
        for d in range(2):
            self.dma_in("pool", wl[0][:, :, d * 64:(d + 1) * 64], self.d_lw1[d].rearrange("(kc p) n -> p kc n", p=128))
            self.dma_in("pool", wl[1][:, :, d * 64:(d + 1) * 64], self.d_la1[d].rearrange("(kc p) n -> p kc n", p=128))
        self.dma_in("pool", wl[2].v, self.d_g1.rearrange("(kc p) n -> p kc n", p=128))
        for i, (mi, dst, fn) in enumerate((("w", tw, AF.Tanh), ("a", ta, AF.Copy), ("g", sg, AF.Sigmoid))):
            scale_w(wsc[i], wl[i], mu_idx[mi])
            pb = self.nb()
            proj(pb, wl[i], wsc[i])
            self.act(dst.v, pb.v, fn)

        if KCUT == 1:
            return
        wsrc = [self.d_wrkv[i].rearrange("(kc p) n -> p kc n", p=128) for i in range(3)]

        def load_w(cc):
            sl = (cc + 1) % 2
            for i in range(3):
                self.dma_in("pool", wraw[sl][i].v, wsrc[i][:, :, cc * 128:(cc + 1) * 128])

        projb = [self.ps[0], self.ps[1], self.ps[2]]

        def proj_gen(cc_):
            self.nb_excl = {0, 1, 2}
            wr_ = wraw[(cc_ + 1) % 2]
            for i, nm in enumerate(("r", "k", "v")):
                scale_w(wsc[i], wr_[i], mu_idx[nm])
                for kc in range(KC):
                    self.mm(projb[i].v, wr_[i][:, kc, :], hb[kc][:, 1:513], start=(kc == 0), stop=False)
                    if kc % 4 == 3:
                        yield
                for kc in range(KC):
                    self.mm(projb[i].v, wsc[i][:, kc, :], xx[kc], start=False, stop=(kc == KC - 1))
                    if kc % 4 == 3:
                        yield

        load_w(0)
        for cc in range(8):
            sl = (cc + 1) % 2
            if cc + 1 < 8:
                load_w(cc + 1)
            wr = wraw[sl]
            if cc == 0:
                for _ in proj_gen(0):
                    pass
            self.cp("act", r_sb.v, projb[0].v)
            self.cp("dve", kf.v, projb[1].v)
            self.cp("act", v_sb.v, projb[2].v)
            self.nb_excl = set()
            kkr = f32t[0]
            self.ts("dve", kkr.v, kf.v, self.vcol("k_k", cc), ALU.mult)
            sqk = fmt[0]
            self.act(sqk.v, kkr.v, AF.Square)
            pb = self.nb()
            self.mm(pb.v, self.bones, sqk.v)
            rn = f32t[1]
            self.act(rn.v, pb.v, AF.Sqrt, bias=self.vcol("eps", 2))
            self.recip(rn.v, rn.v)
            self.tt("dve", kk.v, kkr.v, rn.v, ALU.mult)
            for d in kdirs:
                hs = slice(d * 64, (d + 1) * 64)
                if d in dirs:
                    pw_ = self.nb()
                    self.mm(pw_.v, R["w2c"][hs, cc * 128:(cc + 1) * 128], tw[hs, :])
                    sig = f32t[2]
                    self.act(sig.v, pw_.v, AF.Sigmoid, bias=self.vcol("w0", d * 8 + cc))
                    self.ts("dve", sig.v, sig.v, -C_ED, ALU.mult)
                    self.cumsum(G[d][:, 1:513], R["onesf"].v, sig.v)
                pa_ = self.nb()
                self.mm(pa_.v, R["a2c"][hs, cc * 128:(cc + 1) * 128], ta[hs, :])
                av = f32t[3]
                self.act(av.v, pa_.v, AF.Sigmoid, bias=self.vcol("a0", d * 8 + cc))
                t2 = f32t[2]
                self.ts("dve", t2.v, av.v, self.vcol("k_a", cc), ALU.mult, R["omka"][:, cc:cc + 1], ALU.add)
                self.tt("dve", kd[d].v, kf.v, t2.v, ALU.mult)
                if d in dirs:
                    self.tt("dve", bb[d].v, kk.v, av.v, ALU.mult)
            if KCUT == 2:
                continue
            pbt = self.nb()
            pbt16 = pbt.v.bit(BF16)
            for n in range(4):
                self.tr(pbt16[:, n * 128:(n + 1) * 128], v_sb[:, n * 128:(n + 1) * 128], ident)
            self.cp("dve", Vtok.v.re("p a b -> p (a b)"), pbt16[:, 0:512])
            ywritten = [False] * 4
            for d in dirs:
                Gi = G[d][:, 1:513].re("p (n t) -> p n t", t=128)
                Ge = G[d][:, 0:512].re("p (n t) -> p n t", t=128)
                Gbase = Ge[:, :, 0:1].bc([128, 4, 128])
                Gend = Gi[:, :, 127:128].bc([128, 4, 128])
                LA, LB, LC = f32t[0], f32t[1], f32t[2]
                v3 = lambda t_: t_.v.re("p (n t) -> p n t", t=128)
                Eo, Ex, Ee = f32t[3], f32t[0], f32t[1]
                if d == 0:
                    self.tt("dve", v3(LA), Gi, Gbase, ALU.subtract)
                    self.act(Ein[d].v, LA.v, AF.Exp)
                    self.act(Eo.v, LA.v, AF.Exp, scale=-1.0)
                    self.tt("dve", v3(LB), Ge, Gbase, ALU.subtract)
                    self.tt("dve", v3(LC), Gi, Gend, ALU.subtract)
                    sB, sC = 1.0, -1.0
                else:
                    self.tt("dve", v3(LA), Ge, Gend, ALU.subtract)
                    self.act(Ein[d].v, LA.v, AF.Exp, scale=-1.0)
                    self.act(Eo.v, LA.v, AF.Exp)
                    self.tt("dve", v3(LB), Gi, Gend, ALU.subtract)
                    self.tt("dve", v3(LC), Ge, Gbase, ALU.subtract)
                    sB, sC = -1.0, 1.0
                self.tt("dve", rW[d].v, r_sb.v, Ein[d].v, ALU.mult)
                self.tt("dve", kW[d].v, kd[d].v, Eo.v, ALU.mult)
                self.tt("dve", bW[d].v, bb[d].v, Eo.v, ALU.mult)
                self.act(Ex.v, LB.v, AF.Exp, scale=sB)
                self.tt("dve", kkW[d].v, kk.v, Ex.v, ALU.mult)
                self.act(Ee.v, LC.v, AF.Exp, scale=sC)
                self.tt("dve", fmt[0].v, kd[d].v, Ee.v, ALU.mult)
                self.stt(fmt[1].v, bb[d].v, -1.0, Ee.v, ALU.mult, ALU.mult)
                for (srcf, dstt) in ((fmt[0], kWct[d]), (fmt[1], nbWct[d])):
                    pbt = self.nb()
                    pbt16 = pbt.v.bit(BF16)
                    for n in range(4):
                        self.tr(pbt16[:, n * 128:(n + 1) * 128], srcf[:, n * 128:(n + 1) * 128], ident)
                    self.cp("act", dstt.v.re("p a b -> p (a b)"), pbt16[:, 0:512])
                if KCUT == 3:
                    continue
                if d == 0:
                    mS_N, mS_A, mI, mnI = self.mSU, self.mSL, self.mUI, self.mnUI
                else:
                    mS_N, mS_A, mI, mnI = self.mSL, self.mSU, self.mLI, self.mnLI

                def unit_mm(dst, lh, rh, mask, neg_ident=None):
                    for half in range(2):
                        pb_ = self.nb()
                        for u in range(4):
                            n, hh = u, half
                            ps_ = slice(hh * 64, (hh + 1) * 64)
                            self.mm(pb_[:, u * 128:(u + 1) * 128], lh[ps_, n * 128:(n + 1) * 128],
                                    rh[ps_, n * 128:(n + 1) * 128])
                        self.tt("dve", dst.h[half].v,
                                pb_.v.re("p (u t) -> p u t", t=128),
                                mask.re("p (o t) -> p o t", o=1).bc([128, 4, 128]), ALU.mult)

                unit_mm(tb["N0"], bW[d], kkW[d], mS_N)
                unit_mm(tb["A0"], kkW[d], bW[d], mS_A)
                unit_mm(AakT[d], kW[d], kkW[d], mS_N)
                unit_mm(ArkT[d], kW[d], rW[d], mI)
                unit_mm(nArbT[d], bW[d], rW[d], mnI)
                if KCUT == 4:
                    continue
                pA = self.patL if d == 0 else self.patU
                pN = self.patU if d == 0 else self.patL
                b4 = lambda m: m.re("p (o t) -> p o t", o=1).bc([128, 4, 128])

                def tbuild(hf, d=d, pA=pA, pN=pN):
                    ev = "act" if hf == 0 else "dve"
                    Aoffs = [tb["P0"].h[hf], tb["P1"].h[hf]]
                    Zb, nTt, TTd = tb["A1"].h[hf], tb["N1"].h[hf], TT[d].h[hf]
                    A0h, N0h = tb["A0"].h[hf], tb["N0"].h[hf]
                    self.tt("dve", Aoffs[0].v, N0h.v, b4(pN[0]), ALU.mult)
                    self.stt(TTd.v, Aoffs[0].v, -1.0, b4(ident), ALU.mult, ALU.add)
                    self.tt("dve", Zb.v, A0h.v, b4(pA[0]), ALU.mult)
                    self.stt(nTt.v, b4(ident), -1.0, Zb.v, ALU.mult, ALU.add)
                    self.tt("dve", Aoffs[1].v, A0h.v, b4(pA[1]), ALU.mult)
                    yield
                    for li in range(1, 7):
                        Aoff = Aoffs[li % 2]
                        pb_ = self.nb()
                        for u in range(4):
                            self.mm(pb_[:, u * 128:(u + 1) * 128], Aoff[:, u, :], TTd[:, u, :])
                        self.cp(ev, Zb.v, pb_.v.re("p (u t) -> p u t", t=128))
                        if li < 6:
                            self.tt("dve", Aoffs[(li + 1) % 2].v, A0h.v, b4(pA[li + 1]), ALU.mult)
                        yield
                        pb_ = self.nb()
                        for u in range(4):
                            self.mm(pb_[:, u * 128:(u + 1) * 128], ident, TTd[:, u, :], start=True, stop=False)
                            self.mm(pb_[:, u * 128:(u + 1) * 128], nTt[:, u, :], Zb[:, u, :], start=False, stop=True)
                        self.cp(ev, TTd.v, pb_.v.re("p (u t) -> p u t", t=128))
                        yield
                        if li < 6:
                            pbt = self.nb()
                            pbt16 = pbt.v.bit(BF16)
                            for u in range(4):
                                self.tr(pbt16[:, u * 128:(u + 1) * 128], TTd[:, u, :], ident)
                            if hf == 0:
                                self.act(nTt.v.re("p a b -> p (a b)"), pbt16[:, 0:512], AF.Copy, scale=-1.0)
                            else:
                                self.ts("dve", nTt.v.re("p a b -> p (a b)"), pbt16[:, 0:512], -1.0, ALU.mult)
                            yield

                gens = [tbuild(0), tbuild(1)]
                while gens:
                    for gobj in list(gens):
                        try:
                            next(gobj)
                        except StopIteration:
                            gens.remove(gobj)

                if KCUT == 5:
                    continue
                def chain(si, chunks, init, fin, d=d):
                    M = R["M"][si][d][cc]
                    mi = 0
                    if init == "zero":
                        self.memset("dve", M.v, 0.0)
                        self.memset("dve", Mb[si][0].v, 0.0)
                    else:
                        self.cp("dve", Mb[si][0].v, M.v)
                    yield
                    for n in chunks:
                        tok = slice(n * 128, (n + 1) * 128)
                        mb = Mb[si][mi]
                        pR = self.nb()
                        for hh in range(2):
                            ps_ = slice(hh * 64, (hh + 1) * 64)
                            self.mm(pR[:, hh * 64:(hh + 1) * 64], kkW[d][ps_, tok], mb[ps_, :], start=True, stop=False)
                            self.mm(pR[:, hh * 64:(hh + 1) * 64], AakT[d].u(hh * 4 + n),
                                    Vtok[:, n, hh * 64:(hh + 1) * 64], start=False, stop=True)
                        self.cp("act", Rb[si].v, pR[:, 0:128])
                        yield
                        pU = self.nb()
                        for hh in range(2):
                            self.mm(pU[:, hh * 64:(hh + 1) * 64], TT[d].u(hh * 4 + n), Rb[si][:, hh * 64:(hh + 1) * 64])
                        self.cp("dve", Ub[si].v, pU[:, 0:128])
                        yield
                        pM = self.nb()
                        for hh in range(2):
                            ps_ = slice(hh * 64, (hh + 1) * 64)
                            self.mm(pM[ps_, 0:64], kWct[d][:, n, hh * 64:(hh + 1) * 64],
                                    Vtok[:, n, hh * 64:(hh + 1) * 64], start=True, stop=False)
                            self.mm(pM[ps_, 0:64], nbWct[d][:, n, hh * 64:(hh + 1) * 64],
                                    Ub[si][:, hh * 64:(hh + 1) * 64], start=False, stop=True)
                        pY = self.nb()
                        for hh in range(2):
                            ps_ = slice(hh * 64, (hh + 1) * 64)
                            self.mm(pY[ps_, 0:128], mb[ps_, :], rW[d][ps_, tok], start=True, stop=False)
                            self.mm(pY[ps_, 0:128], Vtok[:, n, hh * 64:(hh + 1) * 64], ArkT[d].u(hh * 4 + n),
                                    start=False, stop=False)
                            self.mm(pY[ps_, 0:128], Ub[si][:, hh * 64:(hh + 1) * 64], nArbT[d].u(hh * 4 + n),
                                    start=False, stop=True)
                        dcol = n * 128 + (127 if d == 0 else 0)
                        self.stt(M.v, M.v, Ein[d][:, dcol:dcol + 1], pM[:, 0:64], ALU.mult, ALU.add)
                        mi ^= 1
                        self.cp("dve", Mb[si][mi].v, M.v)
                        if pa["store"] is not None:
                            self.cp("act", R["yfs"][pa["store"]][cc][:, tok], pY[:, 0:128])
                        elif pa["load"] is not None:
                            self.tt("dve", yacc[:, tok], pY[:, 0:128], R["yfs"][pa["load"]][cc][:, tok], ALU.add)
                        elif not ywritten[n]:
                            self.cp("act", yacc[:, tok], pY[:, 0:128])
                            ywritten[n] = True
                        else:
                            self.tt("dve", yacc[:, tok], pY[:, 0:128], yacc[:, tok], ALU.add)
                        yield
                    if fin is not None:
                        self.dma_out("sp", self.d_ns[d, fin, cc], M.v, lane=self.lane_ns[d])
                        yield

                gens = [chain(si, *cfg) for si, cfg in enumerate(pa["chains"][d])]
                if d == dirs[-1] and cc + 1 < 8 and KCUT > 5:
                    gens.append(proj_gen(cc + 1))
                while gens:
                    for gobj in list(gens):
                        try:
                            next(gobj)
                        except StopIteration:
                            gens.remove(gobj)

            if final and KCUT > 6:
                pm = self.nb()
                self.mm(pm.v, self.bonesf, yacc.v)
                yc = f32t[0]
                self.stt(yc.v, pm.v, -1.0 / 64, yacc.v, ALU.mult, ALU.add)
                sq2 = f32t[1]
                self.act(sq2.v, yc.v, AF.Square)
                pv2 = self.nb()
                self.mm(pv2.v, self.bonesf, sq2.v)
                rs = f32t[2]
                self.act(rs.v, pv2.v, AF.Sqrt, bias=self.vcol("eps", 1), scale=1.0 / 64)
                self.recip(rs.v, rs.v)
                self.tt("dve", yc.v, yc.v, rs.v, ALU.mult)
                zz = f32t[3]
                self.act(zz.v, yc.v, AF.Identity, bias=self.vcol("gn_b", cc), scale=self.vcol("gn_w", cc))
                pbn = self.nb()
                for d2 in range(2):
                    self.stt(fmt[d2].v, r_sb.v, self.vcol("r_k", cc), kd[d2].v, ALU.mult, ALU.mult)
                    self.mm(pbn.v, self.bones, fmt[d2].v, start=(d2 == 0), stop=(d2 == 1))
                bon = f32t[1]
                self.tt("dve", bon.v, pbn.v, v_sb.v, ALU.mult)
                self.tt("dve", zz.v, zz.v, bon.v, ALU.add)
                pg = self.nb()
                self.mm(pg.v, R["g2"][:, cc * 128:(cc + 1) * 128], sg.v)
                self.tt("dve", z[cc].v, zz.v, pg.v, ALU.mult)

        if final and KCUT > 7:
            wod = self.d_wo.rearrange("(kc p) n -> p kc n", p=128)
            wts = [wraw[0][0], wraw[0][1], wraw[0][2], wraw[1][0], wraw[1][1], wraw[1][2]]
            for dc in range(KC):
                wt = wts[dc % 6]
                self.dma_in("pool", wt.v, wod[:, :, dc * 128:(dc + 1) * 128])
                pb = self.nb()
                for kc in range(KC):
                    self.mm(pb.v, wt[:, kc, :], z[kc].v, start=(kc == 0), stop=(kc == KC - 1))
                self.cp("dve", osb[dc], pb.v)
            sqs = [z[c].v for c in range(KC)]
            self.rms_rstd(osb, sqs, self.nb().v, rstd.v)
            for c in range(KC):
                tm = f32t[c % 2]
                self.tt("dve", tm.v, osb[c], rstd.v, ALU.mult)
                self.stt(self.XT[c][tt].v, tm.v, self.gp(l, s, c, q), self.XT[c][tt].v, ALU.mult, ALU.add)


def _vec8(v):
    v = np.asarray(v, np.float32)
    lead = int(np.prod(v.shape[:-1])) if v.ndim > 1 else 1
    return np.ascontiguousarray(v.reshape(lead, 8, 128).transpose(2, 0, 1).reshape(128, lead * 8))


def _consts():
    cst = np.zeros((128, CST_N), np.float32)
    i = np.arange(128)
    cst[:, 0:128] = np.eye(128)
    cst[:, 128:256] = 1.0
    cst[:, 256:384] = (i[:, None] // 64 == i[None, :] // 64)
    cst[:, 384:512] = (i[:, None] < i[None, :])
    cst[:, 512:640] = (i[:, None] > i[None, :])
    cst[:, 640:768] = (i[:, None] <= i[None, :])
    cst[:, 768:896] = (i[:, None] >= i[None, :])
    for bi_, B in enumerate((1, 2, 4, 8, 16, 32, 64)):
        t_, s_ = i[:, None], i[None, :]
        pb_ = ((t_ // (2 * B)) == (s_ // (2 * B))) & ((t_ % (2 * B)) >= B) & ((s_ % (2 * B)) < B)
        cst[:, (7 + bi_) * 128:(8 + bi_) * 128] = pb_
        cst[:, (14 + bi_) * 128:(15 + bi_) * 128] = pb_.T
    pc = np.zeros((4, 256 + 32 + 64), np.float32)
    for wi, w in enumerate(POOL_WINDOWS):
        for off, L in ((0, 256), (256, 32), (288, 64)):
            t = np.arange(L)
            lo = np.clip(t - w // 2, 0, L)
            hi = np.clip(t + w - w // 2, 0, L)
            pc[wi, off:off + L] = 1.0 / (hi - lo)
    pcst = np.ascontiguousarray(np.broadcast_to(pc.reshape(1, -1), (128, pc.size))).astype(np.float32)
    return cst, pcst


_STOP = None
KCUT = int(os.environ.get('KCUT', '99'))
KPASS = int(os.environ.get('KPASS', '99'))
K1CORE = int(os.environ.get('K1CORE', '0'))
_NC_CACHE = {}


def kernel(x_prompt, x_sample, c, state_ctx_fwd, state_ctx_bwd, c_ctx, w_mod, b_mod, norm_pre, norm_post,
           ffn_w1, ffn_w3, ffn_w2, rwkv_mu, rwkv_w_rkv, rwkv_w0, rwkv_w1, rwkv_w2, rwkv_a0, rwkv_a1,
           rwkv_a2, rwkv_g1, rwkv_g2, rwkv_k_k, rwkv_k_a, rwkv_r_k, rwkv_gn_w, rwkv_gn_b, rwkv_w_o,
           pool_w, pool_scale):
    f = lambda a: np.ascontiguousarray(np.asarray(a, np.float32))
    ncores = 8
    key = _STOP
    if key not in _NC_CACHE:
        kb = KB(stop=_STOP)
        _NC_CACHE[key] = kb.build()
        _NC_CACHE["kb"] = kb
    nc = _NC_CACHE[key]
    cst, pcst = _consts()
    shared = {
        "cst": cst, "pcst": pcst,
        "w_mod": f(w_mod), "ffn_w1": f(ffn_w1), "ffn_w3": f(ffn_w3), "ffn_w2": f(ffn_w2),
        "rwkv_w_rkv": f(rwkv_w_rkv[0]), "rwkv_w1": f(rwkv_w1[0]), "rwkv_w2": f(rwkv_w2[0]),
        "rwkv_a1": f(rwkv_a1[0]), "rwkv_a2": f(rwkv_a2[0]), "rwkv_g1": f(rwkv_g1[0]),
        "rwkv_g2": f(rwkv_g2[0]), "rwkv_w_o": f(rwkv_w_o[0]), "pool_w": f(pool_w[0]),
    }
    vec_common = {
        "npre": _vec8(norm_pre), "npost": _vec8(norm_post),
        "bmod": np.ascontiguousarray(np.asarray(b_mod, np.float32).reshape(2, 72, 128).transpose(2, 0, 1).reshape(128, 144)),
        "mu": _vec8(rwkv_mu[0]), "w0": _vec8(rwkv_w0[0]), "a0": _vec8(rwkv_a0[0]),
        "k_k": _vec8(rwkv_k_k[0]), "k_a": _vec8(rwkv_k_a[0]), "r_k": _vec8(np.asarray(rwkv_r_k[0]).reshape(-1)),
        "gn_w": _vec8(rwkv_gn_w[0]), "gn_b": _vec8(rwkv_gn_b[0]), "pscale": _vec8(pool_scale[0]),
    }
    eps = np.zeros((128, 8), np.float32)
    eps[:, 0] = RMS_EPS
    eps[:, 1] = GN_EPS
    eps[:, 2] = 1e-24
    in_maps = []
    x_sample = np.asarray(x_sample, np.float32)
    x_prompt = np.asarray(x_prompt, np.float32)
    for b in range(ncores):
        xT = np.empty((D, TTOT), np.float32)
        xT[:, :TS] = x_sample[b].T
        xT[:, TS:] = x_prompt[4 * b:4 * b + 4].reshape(TP, D).T
        vecs = np.zeros((128, NV), np.float32)
        for n, arr in vec_common.items():
            vecs[:, VEC_OFF[n]:VEC_OFF[n] + arr.shape[1]] = arr
        cond = np.stack([np.asarray(c[b], np.float32), np.asarray(c_ctx, np.float32)], axis=0)
        vecs[:, VEC_OFF["cond"]:VEC_OFF["cond"] + 16] = cond.reshape(2, 8, 128).transpose(2, 1, 0).reshape(128, 16)
        vecs[:, VEC_OFF["eps"]:VEC_OFF["eps"] + 8] = eps
        s0 = np.stack([np.asarray(state_ctx_fwd[b, 0], np.float32), np.asarray(state_ctx_bwd[b, 0], np.float32)])
        s0 = s0.transpose(0, 1, 3, 2).reshape(2, 8, 128, 64)
        m = dict(shared)
        m["xT"] = xT
        m["vecs"] = vecs
        m["s0"] = np.ascontiguousarray(s0)
        in_maps.append(m)
    if K1CORE:
        res1 = run_bass_kernel_spmd(nc, in_maps[:1], core_ids=[0])
        class _R:
            pass
        res = _R()
        res.results = [res1.results[0]] * ncores
    else:
        res = run_bass_kernel_spmd(nc, in_maps, core_ids=list(range(ncores)))
    y_sample = np.empty((8, TS, D), np.float32)
    y_prompt = np.empty((32, 256, D), np.float32)
    nsf = np.empty((32, 1, 16, 64, 64), np.float32)
    nsb = np.empty((32, 1, 16, 64, 64), np.float32)
    for b in range(ncores):
        r = res.results[b]
        yT = r["yT"]
        y_sample[b] = yT[:, :TS].T
        y_prompt[4 * b:4 * b + 4] = yT[:, TS:].T.reshape(4, 256, D)
        ns = r["ns"].reshape(2, 4, 16, 64, 64).transpose(0, 1, 2, 4, 3)
        nsf[4 * b:4 * b + 4, 0] = ns[0]
        nsb[4 * b:4 * b + 4, 0] = ns[1]
    return (y_prompt, y_sample, nsf, nsb)
```

```python
import contextlib
import os
import numpy as np
import concourse.bass as bass
import concourse.mybir as mybir
from concourse.bass_utils import run_bass_kernel_spmd

F32 = mybir.dt.float32
BF16 = mybir.dt.bfloat16
AF = mybir.ActivationFunctionType
ALU = mybir.AluOpType

D = 1024
KC = 8
DFF = 2816
FC = 22
TS = 2048
TP = 1024
TTOT = TS + TP
RMS_EPS = 1e-6
GN_EPS = 64e-5
POOL_WINDOWS = (2, 4, 8, 16)
ENGINES = ("pe", "act", "dve", "pool", "sp")
SAME_ENG_DIST = 10 ** 9


class V:
    __slots__ = ("t", "ap")

    def __init__(self, t, ap):
        self.t = t
        self.ap = ap

    def __getitem__(self, k):
        return V(self.t, self.ap[k])

    def bc(self, shape):
        return V(self.t, self.ap.to_broadcast(list(shape)))

    def re(self, pat, **kw):
        return V(self.t, self.ap.rearrange(pat, **kw))

    def bit(self, dt):
        return V(self.t, self.ap.bitcast(dt))


class T:
    __slots__ = ("ap", "name", "w", "r", "sem", "dcount", "frozen")

    def __init__(self, ap, name=""):
        if ap is not None and not isinstance(ap, bass.AP):
            ap = ap[:]
        self.ap = ap
        self.name = name
        self.w = None
        self.r = []
        self.sem = None
        self.dcount = 0
        self.frozen = False

    def __getitem__(self, k):
        return V(self, self.ap[k])

    @property
    def v(self):
        return V(self, self.ap)


class Op:
    __slots__ = ("eng", "fn", "deps", "dma", "lane", "lane_val", "signaled", "sigval", "pos")

    def __init__(self, eng, fn, dma=False):
        self.eng = eng
        self.fn = fn
        self.deps = []
        self.dma = dma
        self.lane = None
        self.lane_val = 0
        self.signaled = False
        self.sigval = 0


class Prog:
    SEM_LIMIT = 30000

    def __init__(self, nc, stack):
        self.nc = nc
        self.stack = stack
        self.ops = {e: [] for e in ENGINES}
        self.last = {e: None for e in ENGINES}
        self.lane_last = {}
        self.bar_deps = []
        self.final_waits = []
        self.same_engine_sync = {"pe": False, "act": True, "dve": True, "pool": True, "sp": False}
        self.nops = 0
        self.sem_pool = {}
        self.nsem = 0

    def new_sem(self, name):
        self.nsem += 1
        return self.stack.enter_context(self.nc.semaphore(name))

    def seal(self, ops):
        for op in ops:
            op.lane_val = op.lane.dcount

    def sbuf(self, name, shape, dt):
        return self.stack.enter_context(self.nc.sbuf_tensor("sb_" + name, list(shape), dt))

    def psum(self, name, shape, dt=F32):
        return self.stack.enter_context(self.nc.psum_tensor("pp_" + name, list(shape), dt))

    def add(self, eng, fn, reads=(), writes=(), dma=False, lane_tile=None):
        op = Op(eng, fn, dma)
        deps = []
        for t in reads:
            if t.w is not None:
                deps.append(t.w)
        for t in writes:
            if t.w is not None:
                deps.append(t.w)
            deps.extend(t.r)
        op.deps = deps
        for d in deps:
            if not d.dma:
                d.signaled = True
        if dma:
            lt = lane_tile
            if lt.sem is None:
                if lt.name in self.sem_pool:
                    lt.sem, lt.dcount = self.sem_pool[lt.name]
                else:
                    lt.sem = self.new_sem("d_" + lt.name)
            lt.dcount += 16
            self.sem_pool[lt.name] = (lt.sem, lt.dcount)
            op.lane = lt
            op.lane_val = lt.dcount
            self.lane_last[id(lt)] = op
        else:
            self.last[eng] = op
        for t in reads:
            if not t.frozen:
                t.r.append(op)
        for t in writes:
            t.w = op
            t.r = []
        self.ops[eng].append(op)
        self.nops += 1
        return op

    def barrier(self, extra_tiles=()):
        deps = []
        for e in ENGINES:
            if self.last[e] is not None:
                self.last[e].signaled = True
                deps.append(self.last[e])
        deps.extend(self.lane_last.values())
        self.lane_last = {}
        self.bar_deps = deps
        for t in extra_tiles:
            t.r.extend(deps)

    def finish(self, block):
        engsems = {}
        L = self.SEM_LIMIT
        for e in ENGINES:
            c = 0
            for op in self.ops[e]:
                if op.dma:
                    continue
                if op.signaled:
                    c += 1
                    op.sigval = c
            nsem = max(1, (c + L - 1) // L)
            engsems[e] = [self.new_sem(f"e_{e}{i}") for i in range(nsem)]
        self.engsems = engsems
        same = self.same_engine_sync
        final_waits = self.final_waits

        for e in ENGINES:
            c = 0
            for op in self.ops[e]:
                if not op.dma:
                    c += 1
                op.pos = c

        def emit_engine(e, eng):
            known = {}
            for op in self.ops[e]:
                need = {}
                for d in op.deps:
                    if d.dma:
                        key = ("l", id(d.lane))
                        val = d.lane_val
                        sem = d.lane.sem
                    else:
                        if d.eng == e and not same[e]:
                            continue
                        if d.eng == e and (not op.dma) and (op.pos - d.pos >= SAME_ENG_DIST):
                            continue
                        si = (d.sigval - 1) // L
                        key = ("e", d.eng, si)
                        val = d.sigval - si * L
                        sem = engsems[d.eng][si]
                    if known.get(key, 0) >= val:
                        continue
                    if key not in need or need[key][1] < val:
                        need[key] = (sem, val)
                for key, (sem, val) in need.items():
                    eng.wait_ge(sem, val)
                    known[key] = val
                ins = op.fn(eng)
                if op.dma:
                    ins.then_inc(op.lane.sem, 16)
                elif op.signaled:
                    si = (op.sigval - 1) // L
                    ins.then_inc(engsems[e][si], 1)
            if e == "sp":
                for (sem, val) in final_waits:
                    eng.wait_ge(sem, val)

        @block.tensor
        def _(eng):
            emit_engine("pe", eng)

        @block.scalar
        def _(eng):
            emit_engine("act", eng)

        @block.vector
        def _(eng):
            emit_engine("dve", eng)

        @block.gpsimd
        def _(eng):
            emit_engine("pool", eng)

        @block.sync
        def _(eng):
            emit_engine("sp", eng)

    def wait_final(self, tile):
        if tile.sem is not None:
            self.final_waits.append((tile.sem, tile.dcount))


def _tl(*vs):
    return [x.t for x in vs if isinstance(x, V)]


def _a(x):
    return x.ap if isinstance(x, V) else x


class Arena:
    def __init__(self, P, nbytes):
        self.P = P
        self.nbytes = nbytes
        self.raw = P.sbuf("arena", [128, nbytes // 2], BF16)
        self.ptr = 0
        self.hi = 0
        self.where = {}

    def reset(self, base):
        self.ptr = base

    def alloc(self, name, shape, dt):
        n = 1
        for s in shape:
            n *= s
        sz = n * (4 if dt == F32 else 2)
        sz = (sz + 63) // 64 * 64
        off = self.ptr
        assert off + sz <= self.nbytes, f"arena overflow at {name}: {off}+{sz} > {self.nbytes}"
        self.ptr += sz
        self.hi = max(self.hi, self.ptr)
        a = self.raw[:, off // 2: off // 2 + (n * (2 if dt == F32 else 1))]
        if dt == F32:
            a = a.bitcast(F32)
        if len(shape) == 2:
            a = a.rearrange("p (a b) -> p a b", a=shape[0])
        elif len(shape) == 3:
            a = a.rearrange("p (a b c) -> p a b c", a=shape[0], b=shape[1])
        t = T(a, name)
        t.r = list(self.P.bar_deps)
        self.where[name] = (off, tuple(shape), dt)
        return t


VEC_LAYOUT = [
    ("npre", 48), ("npost", 48), ("bmod", 144), ("mu", 48), ("w0", 16), ("a0", 16),
    ("k_k", 8), ("k_a", 8), ("r_k", 8), ("gn_w", 8), ("gn_b", 8), ("pscale", 8), ("cond", 16),
    ("eps", 8),
]
VEC_OFF = {}
_o = 0
for _n, _w in VEC_LAYOUT:
    VEC_OFF[_n] = _o
    _o += _w
NV = _o
CST_N = 21 * 128
ARENA_BYTES = 198 * 1024
XBYTES = KC * TS * 4


class KB:
    def __init__(self, stop=None):
        self.stop = stop
        self.nc = bass.Bass("TRN2", target_bir_lowering=False)
        self.stack = contextlib.ExitStack()

    def mm(self, out, lhsT, rhs, start=True, stop=True):
        o, l, r = out.ap, lhsT.ap, rhs.ap
        self.P.add("pe", lambda e: e.matmul(o, lhsT=l, rhs=r, start=start, stop=stop),
                   [lhsT.t, rhs.t], [out.t])

    def tr(self, out, in_, ident):
        o, i, d = out.ap, in_.ap, ident.ap
        self.P.add("pe", lambda e: e.transpose(o, i, d), [in_.t, ident.t], [out.t])

    def act(self, out, in_, func, bias=None, scale=1.0):
        o, i, b, s = out.ap, in_.ap, _a(bias), _a(scale)
        reads = _tl(in_, bias, scale)
        if bias is None:
            self.P.add("act", lambda e: e.activation(out=o, in_=i, func=func, scale=s), reads, [out.t])
        else:
            self.P.add("act", lambda e: e.activation(out=o, in_=i, func=func, bias=b, scale=s), reads, [out.t])

    def tt(self, eng, out, a, b, op):
        o, x, y = out.ap, a.ap, b.ap
        self.P.add(eng, lambda e: e.tensor_tensor(out=o, in0=x, in1=y, op=op), [a.t, b.t], [out.t])

    def ts(self, eng, out, a, s1, op0, s2=None, op1=None):
        o, x, p1, p2 = out.ap, a.ap, _a(s1), _a(s2)
        reads = _tl(a, s1, s2)
        if op1 is None:
            self.P.add(eng, lambda e: e.tensor_single_scalar(out=o, in_=x, scalar=p1, op=op0), reads, [out.t])
        else:
            self.P.add(eng, lambda e: e.tensor_scalar(out=o, in0=x, scalar1=p1, scalar2=p2, op0=op0, op1=op1),
                       reads, [out.t])

    def stt(self, out, a, scalar, b, op0, op1):
        o, x, s, y = out.ap, a.ap, _a(scalar), b.ap
        self.P.add("dve", lambda e: e.scalar_tensor_tensor(out=o, in0=x, scalar=s, in1=y, op0=op0, op1=op1),
                   _tl(a, scalar, b), [out.t])

    def cp(self, eng, out, in_):
        o, i = out.ap, in_.ap
        if eng == "act":
            self.P.add("act", lambda e: e.activation(out=o, in_=i, func=AF.Copy), [in_.t], [out.t])
        else:
            self.P.add(eng, lambda e: e.tensor_copy(out=o, in_=i), [in_.t], [out.t])

    def rsqrt(self, out, in_, eps, scale=1.0):
        self.act(out, in_, AF.Ln, bias=eps, scale=scale)
        self.act(out, out, AF.Exp, scale=-0.5)

    def recip(self, out, in_):
        o, i = out.ap, in_.ap
        self.P.add("dve", lambda e: e.reciprocal(out=o, in_=i), [in_.t], [out.t])

    def memset(self, eng, out, val):
        o = out.ap
        self.P.add(eng, lambda e: e.memset(o, val), [], [out.t])

    def cumsum(self, out, ones, data):
        o, a, b = out.ap, ones.ap, data.ap
        self.P.add("dve", lambda e: e.tensor_tensor_scan(out=o, data0=a, data1=b, initial=0.0,
                                                        op0=ALU.mult, op1=ALU.add),
                   [ones.t, data.t], [out.t])

    def dma_in(self, eng, out, dram_ap, lane=None):
        o = out.ap
        return self.P.add(eng, lambda e: e.dma_start(out=o, in_=dram_ap), [], [out.t], dma=True,
                          lane_tile=(lane if lane is not None else out.t))

    def dma_out(self, eng, dram_ap, in_, lane=None):
        i = in_.ap
        return self.P.add(eng, lambda e: e.dma_start(out=dram_ap, in_=i), [in_.t], [], dma=True,
                          lane_tile=(lane if lane is not None else in_.t))

    def build(self):
        nc = self.nc
        st = self.stack
        with st:
            self.P = P = Prog(nc, st)
            di = lambda name, shape: nc.dram_tensor(name, list(shape), F32, kind="ExternalInput").ap()
            do = lambda name, shape: nc.dram_tensor(name, list(shape), F32, kind="ExternalOutput").ap()
            self.d_xT = di("xT", [D, TTOT])
            self.d_vecs = di("vecs", [128, NV])
            self.d_cst = di("cst", [128, CST_N])
            self.d_pcst = di("pcst", [128, 4 * (256 + 32 + 64)])
            self.d_s0 = di("s0", [2, 8, 128, 64])
            self.d_wmod = di("w_mod", [2, D, 9 * D])
            self.d_w1 = di("ffn_w1", [2, 2, D, DFF])
            self.d_w3 = di("ffn_w3", [2, 2, D, DFF])
            self.d_w2 = di("ffn_w2", [2, 2, DFF, D])
            self.d_wrkv = di("rwkv_w_rkv", [3, D, D])
            self.d_lw1 = di("rwkv_w1", [2, D, 64])
            self.d_lw2 = di("rwkv_w2", [2, 64, D])
            self.d_la1 = di("rwkv_a1", [2, D, 64])
            self.d_la2 = di("rwkv_a2", [2, 64, D])
            self.d_g1 = di("rwkv_g1", [D, 128])
            self.d_g2 = di("rwkv_g2", [128, D])
            self.d_wo = di("rwkv_w_o", [D, D])
            self.d_pw = di("pool_w", [4, 256, 256])
            self.d_yT = do("yT", [D, TTOT])
            self.d_ns = do("ns", [2, 4, 8, 128, 64])

            self.vecs = T(P.sbuf("vecs", [128, NV], F32), "vecs")
            self.cstb = T(P.sbuf("cstb", [128, CST_N], BF16), "cstb")
            self.bonesf_t = T(P.sbuf("bonesf", [128, 128], F32), "bonesf")
            self.maskn = T(P.sbuf("maskn", [128, 2, 128], BF16), "maskn")
            self.modv = T(P.sbuf("modv", [128, 288], F32), "modv")
            self.gsv = T(P.sbuf("gsv", [128, 96], F32), "gsv")
            self.gpv = T(P.sbuf("gpv", [128, 96], F32), "gpv")
            self.scb = T(P.sbuf("scb", [128, 16], BF16), "scb")
            self.ps = [T(P.psum(f"ps{i}", [128, 512], F32), f"ps{i}") for i in range(8)]
            self.modtmp = T(P.sbuf("modtmp", [128, 16], F32), "modtmp")
            self.A = Arena(P, ARENA_BYTES)
            X = self.A.raw[:, 0:XBYTES // 2].bitcast(F32).rearrange("p (c t) -> p c t", c=KC)
            self.Xap = X
            self.XT = [[T(X[:, c, tt * 512:(tt + 1) * 512], f"X{c}_{tt}") for tt in range(4)] for c in range(KC)]

            self.lane_x = [T(None, f"lx{i}") for i in range(4)]
            self.lane_y = T(None, "ly")
            self.lane_ns = [T(None, "lns0"), T(None, "lns1")]
            self.setup()
            phases = [
                dict(name="s", tok0=0, T=TS, q=0, grid=True, seqs=[(0, TS)]),
                dict(name="p", tok0=TS, T=TP, q=1, grid=False, seqs=[(i * 256, 256) for i in range(4)]),
            ]
            for ph in phases:
                self.run_phase(ph)
            with nc.Block() as block:
                P.finish(block)
        return nc

    def vcol(self, name, idx, n=1):
        o = VEC_OFF[name] + idx
        return self.vecs[:, o:o + n]

    def setup(self):
        P = self.P
        self.dma_in("sp", self.vecs.v, self.d_vecs[:, :])
        self.dma_in("pool", self.cstb.v, self.d_cst[:, :])
        self.dma_in("sp", self.bonesf_t.v, self.d_cst[:, 256:384])
        self.vecs.frozen = True
        for m in range(2):
            self.ts("dve", self.maskn[:, m, :], self.cstb[:, (5 + m) * 128:(6 + m) * 128], -1.0, ALU.mult)
        self.cstb.frozen = True
        self.bonesf_t.frozen = True
        self.maskn.frozen = True
        self.ident = self.cstb[:, 0:128]
        self.ones = self.cstb[:, 128:256]
        self.bones = self.cstb[:, 256:384]
        self.bonesf = self.bonesf_t.v
        self.mSU = self.cstb[:, 384:512]
        self.mSL = self.cstb[:, 512:640]
        self.mUI = self.cstb[:, 640:768]
        self.mLI = self.cstb[:, 768:896]
        self.mnUI = self.maskn[:, 0, :]
        self.mnLI = self.maskn[:, 1, :]
        self.patL = [self.cstb[:, (7 + i) * 128:(8 + i) * 128] for i in range(7)]
        self.patU = [self.cstb[:, (14 + i) * 128:(15 + i) * 128] for i in range(7)]
        self.act(self.scb.v, self.vcol("cond", 0, 16), AF.Silu)
        A = self.A
        A.reset(XBYTES)
        P.barrier()
        wslots = [A.alloc(f"wm{i}", [8, 1024], BF16) for i in range(2)]
        psM = self.ps[7]
        cnt = 0
        for l in range(2):
            wm = self.d_wmod[l].rearrange("(kc p) n -> p kc n", p=128)
            for blk in range(9):
                ws = wslots[cnt % 2]
                cnt += 1
                self.dma_in("pool", ws.v, wm[:, :, blk * 1024:(blk + 1) * 1024])
                for jj in range(8):
                    j = blk * 8 + jj
                    col = (l * 72 + j) * 2
                    for kc in range(8):
                        self.mm(psM[:, col:col + 2], ws[:, kc, jj * 128:(jj + 1) * 128],
                                self.scb[:, kc * 2:kc * 2 + 2], start=(kc == 0), stop=(kc == 7))
        for l in range(2):
            bm = self.vcol("bmod", l * 72, 72)
            self.tt("dve", self.modv[:, l * 144:(l + 1) * 144].re("p (j q) -> p j q", q=2),
                    psM[:, l * 144:(l + 1) * 144].re("p (j q) -> p j q", q=2),
                    bm.re("p (j o) -> p j o", o=1).bc([128, 72, 2]), ALU.add)
        tmp = self.modtmp
        for l in range(2):
            for s in range(3):
                base = l * 144 + (3 * s) * 16
                o = (l * 3 + s) * 16
                npre = self.vcol("npre", (l * 3 + s) * 8, 8).re("p (c o) -> p c o", o=1).bc([128, 8, 2])
                npost = self.vcol("npost", (l * 3 + s) * 8, 8).re("p (c o) -> p c o", o=1).bc([128, 8, 2])
                self.ts("dve", tmp[:, :], self.modv[:, base + 16:base + 32], 1.0, ALU.add)
                self.tt("dve", self.gsv[:, o:o + 16].re("p (c q) -> p c q", q=2),
                        tmp[:, :].re("p (c q) -> p c q", q=2), npre, ALU.mult)
                self.tt("dve", tmp[:, :].re("p (c q) -> p c q", q=2),
                        self.modv[:, base + 32:base + 48].re("p (c q) -> p c q", q=2), npost, ALU.mult)
                self.ts("dve", self.gpv[:, o:o + 16], tmp[:, :], 0.5 if s != 1 else 1.0, ALU.mult)
        self.modv.frozen = True
        self.gsv.frozen = True
        self.gpv.frozen = True

    def gs(self, l, s, c, q):
        o = (l * 3 + s) * 16 + c * 2 + q
        return self.gsv[:, o:o + 1]

    def gp(self, l, s, c, q):
        o = (l * 3 + s) * 16 + c * 2 + q
        return self.gpv[:, o:o + 1]

    def shift(self, l, s, c, q):
        o = l * 144 + (3 * s) * 16 + c * 2 + q
        return self.modv[:, o:o + 1]

    def run_phase(self, ph):
        P = self.P
        Tn = ph["T"]
        ph["NT"] = Tn // 512
        xall = [t for c in range(KC) for t in self.XT[c]]
        P.barrier(extra_tiles=xall)
        for tt in range(ph["NT"]):
            ops = []
            for c in range(KC):
                ops.append(self.dma_in("sp", self.XT[c][tt].v,
                                       self.d_xT[c * 128:(c + 1) * 128,
                                                 ph["tok0"] + tt * 512: ph["tok0"] + (tt + 1) * 512],
                                       lane=self.lane_x[tt]))
            P.seal(ops)
        stages = [("ffn", 0, 0), ("rwkv", 0, 1), ("ffn", 0, 2), ("ffn", 1, 0), ("pool", 1, 1), ("ffn", 1, 2)]
        nst = len(stages) if self.stop is None else self.stop
        for (kind, l, s) in stages[:nst]:
            if kind == "ffn":
                self.ffn(ph, l, s)
            elif kind == "rwkv":
                self.rwkv(ph, l, s)
            else:
                self.pool(ph, l, s)
        ops = []
        for c in range(KC):
            for tt in range(ph["NT"]):
                ops.append(self.dma_out("sp", self.d_yT[c * 128:(c + 1) * 128,
                                                        ph["tok0"] + tt * 512: ph["tok0"] + (tt + 1) * 512],
                                        self.XT[c][tt].v, lane=self.lane_y))
        P.seal(ops)
        P.wait_final(self.lane_y)

    def rms_rstd(self, srcs, sqs, psq, rstd, n=512, eps_idx=0):
        for c in range(KC):
            self.act(sqs[c], srcs[c], AF.Square)
            self.mm(psq, self.ones, sqs[c], start=(c == 0), stop=(c == KC - 1))
        self.rsqrt(rstd, psq, self.vcol("eps", eps_idx), scale=1.0 / D)

    def ffn(self, ph, l, s):
        P, A = self.P, self.A
        q = ph["q"]
        si = 0 if s == 0 else 1
        P.barrier()
        A.reset(XBYTES)
        h = [[A.alloc(f"h{c}_{t}", [512], BF16) for t in range(2)] for c in range(KC)]
        g = [[A.alloc(f"g{f}_{t}", [512], BF16) for t in range(2)] for f in range(FC)]
        w1s = [A.alloc(f"w1s{i}", [8, 256], BF16) for i in range(2)]
        w3s = [A.alloc(f"w3s{i}", [8, 256], BF16) for i in range(2)]
        w2s = [A.alloc(f"w2s{i}", [FC, 128], BF16) for i in range(2)]
        rstd = [A.alloc(f"rstd{i}", [512], F32) for i in range(2)]
        tmp = [A.alloc(f"tmp{i}", [512], F32) for i in range(2)]
        s1 = [A.alloc(f"s1_{i}", [512], BF16) for i in range(2)]
        fsb = [[A.alloc(f"fsb{c}_{t}", [512], F32) for t in range(2)] for c in range(KC)]
        w1d = self.d_w1[l, si].rearrange("(kc p) f -> p kc f", p=128)
        w3d = self.d_w3[l, si].rearrange("(kc p) f -> p kc f", p=128)
        w2d = self.d_w2[l, si].rearrange("(fc p) d -> p fc d", p=128)
        wcnt = 0
        w2cnt = 0
        pcnt = 0
        ocnt = 0
        tcnt = 0
        for grp in range(ph["T"] // 1024):
            for t in range(2):
                tt = grp * 2 + t
                srcs = [self.XT[c][tt].v for c in range(KC)]
                sqs = [fsb[c][t].v.bit(BF16)[:, 0:512] for c in range(KC)]
                rs = rstd[t]
                self.rms_rstd(srcs, sqs, self.ps[6].v, rs.v)
                for c in range(KC):
                    tm = tmp[tcnt % 2]
                    tcnt += 1
                    self.tt("dve", tm.v, srcs[c], rs.v, ALU.mult)
                    self.act(h[c][t].v, tm.v, AF.Identity, bias=self.shift(l, s, c, q), scale=self.gs(l, s, c, q))
            for fb in range(FC // 2):
                slot = wcnt % 2
                wcnt += 1
                self.dma_in("pool", w1s[slot].v, w1d[:, :, fb * 256:(fb + 1) * 256])
                self.dma_in("pool", w3s[slot].v, w3d[:, :, fb * 256:(fb + 1) * 256])
                for t in range(2):
                    for fi in range(2):
                        fc = fb * 2 + fi
                        pu1 = self.ps[(pcnt % 2) * 2]
                        pu3 = self.ps[(pcnt % 2) * 2 + 1]
                        sb = s1[pcnt % 2]
                        pcnt += 1
                        for kc in range(KC):
                            self.mm(pu1.v, w1s[slot][:, kc, fi * 128:(fi + 1) * 128], h[kc][t].v,
                                    start=(kc == 0), stop=(kc == KC - 1))
                        for kc in range(KC):
                            self.mm(pu3.v, w3s[slot][:, kc, fi * 128:(fi + 1) * 128], h[kc][t].v,
                                    start=(kc == 0), stop=(kc == KC - 1))
                        self.act(sb.v, pu1.v, AF.Silu)
                        self.tt("dve", g[fc][t].v, sb.v, pu3.v, ALU.mult)
            for dc in range(KC):
                slot = w2cnt % 2
                w2cnt += 1
                self.dma_in("pool", w2s[slot][:, 0:11, :], w2d[:, 0:11, dc * 128:(dc + 1) * 128])
                self.dma_in("pool", w2s[slot][:, 11:22, :], w2d[:, 11:22, dc * 128:(dc + 1) * 128])
                for t in range(2):
                    po = self.ps[4 + ocnt % 2]
                    ocnt += 1
                    for fc in range(FC):
                        self.mm(po.v, w2s[slot][:, fc, :], g[fc][t].v, start=(fc == 0), stop=(fc == FC - 1))
                    self.cp("dve", fsb[dc][t].v, po.v)
            for t in range(2):
                tt = grp * 2 + t
                srcs = [fsb[c][t].v for c in range(KC)]
                sqs = [h[c][t].v for c in range(KC)]
                rs = rstd[t]
                self.rms_rstd(srcs, sqs, self.ps[6].v, rs.v)
                for c in range(KC):
                    tm = tmp[tcnt % 2]
                    tcnt += 1
                    self.tt("dve", tm.v, srcs[c], rs.v, ALU.mult)
                    self.stt(self.XT[c][tt].v, tm.v, self.gp(l, s, c, q), self.XT[c][tt].v, ALU.mult, ALU.add)

    def pool(self, ph, l, s):
        P, A = self.P, self.A
        q = ph["q"]
        Tn = ph["T"]
        NT = ph["NT"]
        P.barrier()
        A.reset(XBYTES)
        if ph["grid"]:
            Ad, Bd, padA = 32, 64, 8
        else:
            Ad, Bd, padA = Tn // 256, 256, 0
        padB = 8
        RA, RB = Ad + 2 * padA, Bd + 2 * padB
        rpt = 512 // Bd
        rstd = A.alloc("p_rstd", [Tn], F32)
        D0 = A.alloc("p_d0", [RA, RB], F32)
        Pa = A.alloc("p_pa", [RA, RB], F32)
        Pb = A.alloc("p_pb", [RA, RB], F32)
        dlt = [A.alloc(f"p_dlt{i}", [2048], BF16) for i in range(2)]
        outs = [A.alloc(f"p_out{c}", [Tn], F32) for c in range(KC)]
        pws = [A.alloc(f"p_w{i}", [2, 256], BF16) for i in range(2)]
        pcw = 96 if ph["grid"] else 256
        pc = A.alloc("p_cst", [4, pcw], F32)
        tmp = [A.alloc(f"p_tmp{i}", [512], F32) for i in range(2)]
        sqs = [dlt[c // 4][:, (c % 4) * 512:(c % 4 + 1) * 512] for c in range(KC)]
        pcd = self.d_pcst.rearrange("p (w x) -> p w x", w=4)
        self.dma_in("sp", pc.v, pcd[:, :, 256:352] if ph["grid"] else pcd[:, :, 0:256])
        self.memset("dve", D0.v, 0.0)
        for tt in range(NT):
            srcs = [self.XT[c][tt].v for c in range(KC)]
            self.rms_rstd(srcs, sqs, self.ps[6].v, rstd[:, tt * 512:(tt + 1) * 512])
        tcnt = 0
        mcnt = 0
        for c in range(KC):
            gi = c // 2
            w = POOL_WINDOWS[gi]
            hw = w // 2
            L = w.bit_length() - 1
            eng = "dve" if c % 2 == 0 else "pool"
            for tt in range(NT):
                tm = tmp[tcnt % 2]
                tcnt += 1
                self.tt("dve", tm.v, self.XT[c][tt].v, rstd[:, tt * 512:(tt + 1) * 512], ALU.mult)
                self.act(D0[:, padA + tt * rpt: padA + (tt + 1) * rpt, padB:padB + Bd],
                         tm.v.re("p (a b) -> p a b", b=Bd), AF.Identity,
                         bias=self.shift(l, s, c, q), scale=self.gs(l, s, c, q))
            src = D0
            bufs = [Pa, Pb]
            bi = 0
            n = RB
            for m in range(L):
                sh = 1 << m
                n -= sh
                dst = bufs[bi]
                bi ^= 1
                self.tt(eng, dst[:, :, 0:n], src[:, :, 0:n], src[:, :, sh:sh + n], ALU.add)
                src = dst
            c0 = padB - hw
            if ph["grid"]:
                nr = RA
                for m in range(L):
                    sh = 1 << m
                    nr -= sh
                    dst = bufs[bi]
                    bi ^= 1
                    self.tt(eng, dst[:, 0:nr, c0:c0 + Bd], src[:, 0:nr, c0:c0 + Bd],
                            src[:, sh:sh + nr, c0:c0 + Bd], ALU.add)
                    src = dst
                res = src[:, padA - hw: padA - hw + Ad, c0:c0 + Bd]
            else:
                res = src[:, :, c0:c0 + Bd]
            oth = bufs[bi]
            o1 = oth[:, 0:Ad, 0:Bd]
            wi = gi
            if ph["grid"]:
                invA = pc[:, wi, 0:32].re("p (a o) -> p a o", o=1).bc([128, Ad, Bd])
                invB = pc[:, wi, 32:96].re("p (o b) -> p o b", o=1).bc([128, Ad, Bd])
                self.tt(eng, o1, res, invB, ALU.mult)
                self.tt(eng, o1, o1, invA, ALU.mult)
            else:
                invB = pc[:, wi, 0:256].re("p (o b) -> p o b", o=1).bc([128, Ad, Bd])
                self.tt(eng, o1, res, invB, ALU.mult)
            self.tt(eng, dlt[c % 2][:, 0:Tn].re("p (a b) -> p a b", b=Bd), o1,
                    D0[:, padA:padA + Ad, padB:padB + Bd], ALU.subtract)
            if c % 2 == 1:
                pw = pws[gi % 2]
                self.dma_in("pool", pw.v, self.d_pw[gi].rearrange("(kc p) e -> p kc e", p=128))
                for tt in range(NT):
                    for ec in range(2):
                        pb = self.ps[mcnt % 4]
                        mcnt += 1
                        for kc in range(2):
                            self.mm(pb.v, pw[:, kc, ec * 128:(ec + 1) * 128],
                                    dlt[kc][:, tt * 512:(tt + 1) * 512], start=(kc == 0), stop=(kc == 1))
                        oc = 2 * gi + ec
                        self.act(outs[oc][:, tt * 512:(tt + 1) * 512], pb.v, AF.Identity,
                                 scale=self.vcol("pscale", oc))
        for tt in range(NT):
            srcs = [outs[c][:, tt * 512:(tt + 1) * 512] for c in range(KC)]
            rs = rstd[:, tt * 512:(tt + 1) * 512]
            self.rms_rstd(srcs, sqs, self.ps[6].v, rs)
            for c in range(KC):
                tm = tmp[tcnt % 2]
                tcnt += 1
                self.tt("dve", tm.v, srcs[c], rs, ALU.mult)
                self.stt(self.XT[c][tt].v, tm.v, self.gp(l, s, c, q), self.XT[c][tt].v, ALU.mult, ALU.add)

    def nb(self):
        while True:
            i = self.bcnt % 8
            self.bcnt += 1
            if i not in self.nb_excl:
                return self.ps[i]

    def rwkv(self, ph, l, s):
        P, A = self.P, self.A
        self.bcnt = 0
        self.nb_excl = set()
        P.barrier()
        A.reset(XBYTES)
        sample = ph["grid"]
        R = {}
        self.R = R
        R["w2c"] = A.alloc("r_w2c", [1024], BF16)
        R["a2c"] = A.alloc("r_a2c", [1024], BF16)
        R["g2"] = A.alloc("r_g2", [1024], BF16)
        for d in range(2):
            self.dma_in("pool", R["w2c"][d * 64:(d + 1) * 64, :], self.d_lw2[d])
            self.dma_in("pool", R["a2c"][d * 64:(d + 1) * 64, :], self.d_la2[d])
        self.dma_in("pool", R["g2"].v, self.d_g2[:, :])
        R["omka"] = A.alloc("r_omka", [8], F32)
        self.ts("dve", R["omka"].v, self.vcol("k_a", 0, 8), -1.0, ALU.mult, 1.0, ALU.add)
        R["onesf"] = A.alloc("r_onesf", [512], F32)
        self.memset("dve", R["onesf"].v, 1.0)
        nsl = 1 if sample else 2
        R["M"] = [[[A.alloc(f"r_M{si}_{d}_{cc}", [64], F32) for cc in range(8)] for d in range(2)] for si in range(nsl)]
        if sample:
            R["xhs"] = A.alloc("r_xhs", [8, 8], F32)
            for c in range(KC):
                for qq in range(1, 4):
                    self.cp("dve", R["xhs"][:, c, 2 * (qq - 1):2 * (qq - 1) + 1], self.XT[c][qq - 1][:, 511:512])
                    self.cp("dve", R["xhs"][:, c, 2 * (qq - 1) + 1:2 * (qq - 1) + 2], self.XT[c][qq][:, 0:1])
            R["yfs"] = [[A.alloc(f"r_yf{qq}_{cc}", [512], BF16) for cc in range(8)] for qq in range(3)]
            for d in range(2):
                for cc in range(8):
                    self.dma_in("sp", R["M"][0][d][cc].v, self.d_s0[d, cc])
            passes = []
            for qq in range(3):
                passes.append(dict(p0=qq * 512, dirs=[0], chains={0: [([0, 1, 2, 3], "carry", None)]},
                                   haloL=(qq > 0), haloR=True, breaks=[], store=qq, load=None, final=False))
            passes.append(dict(p0=3 * 512, dirs=[0, 1],
                               chains={0: [([0, 1, 2, 3], "carry", None)], 1: [([3, 2, 1, 0], "carry", None)]},
                               haloL=True, haloR=False, breaks=[], store=None, load=None, final=True))
            for qq in (2, 1, 0):
                passes.append(dict(p0=qq * 512, dirs=[1], chains={1: [([3, 2, 1, 0], "carry", None)]},
                                   haloL=(qq > 0), haloR=True, breaks=[], store=None, load=qq, final=True))
        else:
            passes = []
            for pi in range(2):
                passes.append(dict(p0=pi * 512, dirs=[0, 1],
                                   chains={0: [([0, 1], "zero", 2 * pi), ([2, 3], "zero", 2 * pi + 1)],
                                           1: [([1, 0], "zero", 2 * pi), ([3, 2], "zero", 2 * pi + 1)]},
                                   haloL=False, haloR=False, breaks=[256], store=None, load=None, final=True))
        mark = A.ptr
        for pa in passes[:KPASS]:
            P.barrier()
            A.reset(mark)
            self.rwkv_pass(ph, l, s, pa)
        if sample:
            self.where_s = dict(A.where)
        if not sample:
            P.wait_final(self.lane_ns[0])
            P.wait_final(self.lane_ns[1])

    def rwkv_pass(self, ph, l, s, pa):
        P, A, R = self.P, self.A, self.R
        q = ph["q"]
        p0 = pa["p0"]
        tt = p0 // 512
        dirs = pa["dirs"]
        final = pa["final"]
        kdirs = [0, 1] if final else dirs
        ident = self.ident
        C_ED = 0.6065306597126334

        hx = [A.alloc(f"hx{c}", [1040], BF16) for c in range(KC)]
        hb = [hx[c][:, 0:514] for c in range(KC)]
        xx = [hx[c][:, 528:1040] for c in range(KC)]
        osb = [hx[c].v.bit(F32)[:, 0:512] for c in range(KC)]
        z = [A.alloc(f"z{c}", [512], BF16) for c in range(KC)] if final else None
        tw = A.alloc("tw", [512], BF16)
        ta = A.alloc("ta", [512], BF16)
        sg = A.alloc("sg", [512], BF16)
        wraw = [[A.alloc(f"wraw{sl}_{i}", [8, 128], BF16) for i in range(3)] for sl in range(2)]
        wsc1 = A.alloc("wsc0", [8, 128], BF16)
        wsc = [wsc1, wsc1, wsc1]
        f32t = [A.alloc(f"f32t{i}", [512], F32) for i in range(4)]
        rstd = f32t[3]
        xh = A.alloc("xh", [16], F32)
        xhb = A.alloc("xhb", [16], BF16)
        rstd2 = A.alloc("rstd2", [2], F32)
        r_sb = A.alloc("r_sb", [512], BF16)
        v_sb = A.alloc("v_sb", [512], BF16)
        kk = A.alloc("kk", [512], BF16)
        kf = A.alloc("kf", [512], F32)
        kd = [A.alloc(f"kd{d}", [512], BF16) for d in range(2)]
        bb = [A.alloc(f"bb{d}", [512], BF16) for d in range(2)]
        G = [A.alloc(f"G{d}", [520], F32) for d in range(2)]
        yacc = A.alloc("yacc", [512], F32)
        Ein1 = A.alloc("Ein", [512], F32)
        rW1 = A.alloc("rW", [512], BF16)
        kW1 = A.alloc("kW", [512], BF16)
        bW1 = A.alloc("bW", [512], BF16)
        kkW1 = A.alloc("kkW", [512], BF16)
        Ein, rW, kW, bW, kkW = [Ein1] * 2, [rW1] * 2, [kW1] * 2, [bW1] * 2, [kkW1] * 2
        fmt = [A.alloc(f"fmt{i}", [512], BF16) for i in range(2)]
        Vtok = A.alloc("Vtok", [4, 128], BF16)
        kWct = [A.alloc("kWct", [4, 128], BF16)] * 2
        nbWct = [A.alloc("nbWct", [4, 128], BF16)] * 2
        class H2:
            def __init__(h2, name):
                h2.h = [A.alloc(f"{name}_{i}", [4, 128], BF16) for i in range(2)]

            def u(h2, uu):
                return h2.h[uu // 4][:, uu % 4, :]
        TT = [H2("TT")] * 2
        AakT = [H2("AakT")] * 2
        ArkT = [H2("ArkT")] * 2
        nArbT = [H2("nArbT")] * 2
        tb = {k: H2("tb_" + k) for k in ("A0", "N0", "P0", "P1", "A1", "N1")}
        nsi = max(len(v_) for v_ in pa["chains"].values())
        Mb = [[A.alloc(f"Mb{si}_{i}", [64], BF16) for i in range(2)] for si in range(nsi)]
        Rb = [A.alloc(f"Rb{si}", [128], BF16) for si in range(nsi)]
        Ub = [A.alloc(f"Ub{si}", [128], BF16) for si in range(nsi)]
        for d in range(2):
            self.memset("dve", G[d][:, 0:1], 0.0)

        srcs = [self.XT[c][tt].v for c in range(KC)]
        sqs = [xx[c] for c in range(KC)]
        self.rms_rstd(srcs, sqs, self.nb().v, rstd.v)
        for c in range(KC):
            tm = f32t[c % 2]
            self.tt("dve", tm.v, srcs[c], rstd.v, ALU.mult)
            self.act(hb[c][:, 1:513], tm.v, AF.Identity, bias=self.shift(l, s, c, q), scale=self.gs(l, s, c, q))
        if pa["haloL"] or pa["haloR"]:
            self.memset("dve", xh.v, 1.0)
            for c in range(KC):
                if pa["haloL"]:
                    self.cp("dve", xh[:, 2 * c:2 * c + 1], R["xhs"][:, c, 2 * (tt - 1):2 * (tt - 1) + 1])
                if pa["haloR"]:
                    self.cp("dve", xh[:, 2 * c + 1:2 * c + 2], R["xhs"][:, c, 2 * tt + 1:2 * tt + 2])
            self.act(xhb.v, xh.v, AF.Square)
            pb = self.nb()
            for c in range(KC):
                self.mm(pb[:, 0:2], self.ones, xhb[:, 2 * c:2 * c + 2], start=(c == 0), stop=(c == KC - 1))
            self.rsqrt(rstd2.v, pb[:, 0:2], self.vcol("eps", 0), scale=1.0 / D)
            for c in range(KC):
                self.tt("dve", xh[:, 2 * c:2 * c + 2], xh[:, 2 * c:2 * c + 2], rstd2.v, ALU.mult)
                if pa["haloL"]:
                    self.act(hb[c][:, 0:1], xh[:, 2 * c:2 * c + 1], AF.Identity,
                             bias=self.shift(l, s, c, q), scale=self.gs(l, s, c, q))
                if pa["haloR"]:
                    self.act(hb[c][:, 513:514], xh[:, 2 * c + 1:2 * c + 2], AF.Identity,
                             bias=self.shift(l, s, c, q), scale=self.gs(l, s, c, q))
        for c in range(KC):
            if not pa["haloL"]:
                self.memset("dve", hb[c][:, 0:1], 0.0)
            if not pa["haloR"]:
                self.memset("dve", hb[c][:, 513:514], 0.0)
            tm = f32t[2 + c % 2]
            self.tt("dve", tm.v, hb[c][:, 0:512], hb[c][:, 2:514], ALU.add)
            for bk in pa["breaks"]:
                self.cp("dve", tm[:, bk - 1:bk], hb[c][:, bk - 1:bk])
                self.cp("dve", tm[:, bk:bk + 1], hb[c][:, bk + 2:bk + 3])
            self.stt(xx[c], tm.v, 0.5, hb[c][:, 1:513], ALU.mult, ALU.subtract)

        if KCUT == 0:
            return
        mu_idx = {"r": 0, "w": 1, "k": 2, "v": 3, "a": 4, "g": 5}

        def scale_w(dst, src, mi):
            mu8 = self.vcol("mu", mi * 8, 8).re("p (k o) -> p k o", o=1).bc([128, 8, 128])
            self.tt("dve", dst.v, src.v, mu8, ALU.mult)

        def proj(pb, wr, ws, ncols=128):
            for kc in range(KC):
                self.mm(pb.v, wr[:, kc, 0:ncols], hb[kc][:, 1:513], start=(kc == 0), stop=False)
            for kc in range(KC):
                self.mm(pb.v, ws[:, kc, 0:ncols], xx[kc], start=False, stop=(kc == KC - 1))

        wl = wraw[0]
        for d in range(2):
            self.dma_in("pool", wl[0][:, :, d * 64:(d + 1) * 64], self.d_lw1[d].rearrange("(kc p) n -> p kc n", p=128))
            self.dma_in("pool", wl[1][:, :, d * 64:(d + 1) * 64], self.d_la1[d].rearrange("(kc p) n -> p kc n", p=128))
        self.dma_in("pool", wl[2].v, self.d_g1.rearrange("(kc p) n -> p kc n", p=128))
        for i, (mi, dst, fn) in enumerate((("w", tw, AF.Tanh), ("a", ta, AF.Copy), ("g", sg, AF.Sigmoid))):
            scale_w(wsc[i], wl[i], mu_idx[mi])
            pb = self.nb()
            proj(pb, wl[i], wsc[i])
            self.act(dst.v, pb.v, fn)

        if KCUT == 1:
            return
        wsrc = [self.d_wrkv[i].rearrange("(kc p) n -> p kc n", p=128) for i in range(3)]

        def load_w(cc):
            sl = (cc + 1) % 2
            for i in range(3):
                self.dma_in("pool", wraw[sl][i].v, wsrc[i][:, :, cc * 128:(cc + 1) * 128])

        projb = [self.ps[0], self.ps[1], self.ps[2]]

        def proj_gen(cc_):
            self.nb_excl = {0, 1, 2}
            wr_ = wraw[(cc_ + 1) % 2]
            for i, nm in enumerate(("r", "k", "v")):
                scale_w(wsc[i], wr_[i], mu_idx[nm])
                for kc in range(KC):
                    self.mm(projb[i].v, wr_[i][:, kc, :], hb[kc][:, 1:513], start=(kc == 0), stop=False)
                    if kc % 4 == 3:
                        yield
                for kc in range(KC):
                    self.mm(projb[i].v, wsc[i][:, kc, :], xx[kc], start=False, stop=(kc == KC - 1))
                    if kc % 4 == 3:
                        yield

        load_w(0)
        for cc in range(8):
            sl = (cc + 1) % 2
            if cc + 1 < 8:
                load_w(cc + 1)
            wr = wraw[sl]
            if cc == 0:
                for _ in proj_gen(0):
                    pass
            self.cp("act", r_sb.v, projb[0].v)
            self.cp("dve", kf.v, projb[1].v)
            self.cp("act", v_sb.v, projb[2].v)
            self.nb_excl = set()
            kkr = f32t[0]
            self.ts("dve", kkr.v, kf.v, self.vcol("k_k", cc), ALU.mult)
            sqk = fmt[0]
            self.act(sqk.v, kkr.v, AF.Square)
            pb = self.nb()
            self.mm(pb.v, self.bones, sqk.v)
            rn = f32t[1]
            self.rsqrt(rn.v, pb.v, self.vcol("eps", 2))
            self.tt("dve", kk.v, kkr.v, rn.v, ALU.mult)
            for d in kdirs:
                hs = slice(d * 64, (d + 1) * 64)
                if d in dirs:
                    pw_ = self.nb()
                    self.mm(pw_.v, R["w2c"][hs, cc * 128:(cc + 1) * 128], tw[hs, :])
                    sig = f32t[2]
                    self.act(sig.v, pw_.v, AF.Sigmoid, bias=self.vcol("w0", d * 8 + cc))
                    self.ts("dve", sig.v, sig.v, -C_ED, ALU.mult)
                    self.cumsum(G[d][:, 1:513], R["onesf"].v, sig.v)
                pa_ = self.nb()
                self.mm(pa_.v, R["a2c"][hs, cc * 128:(cc + 1) * 128], ta[hs, :])
                av = f32t[3]
                self.act(av.v, pa_.v, AF.Sigmoid, bias=self.vcol("a0", d * 8 + cc))
                t2 = f32t[2]
                self.ts("dve", t2.v, av.v, self.vcol("k_a", cc), ALU.mult, R["omka"][:, cc:cc + 1], ALU.add)
                self.tt("dve", kd[d].v, kf.v, t2.v, ALU.mult)
                if d in dirs:
                    self.tt("dve", bb[d].v, kk.v, av.v, ALU.mult)
            if KCUT == 2:
                continue
            pbt = self.nb()
            pbt16 = pbt.v.bit(BF16)
            for n in range(4):
                self.tr(pbt16[:, n * 128:(n + 1) * 128], v_sb[:, n * 128:(n + 1) * 128], ident)
            self.cp("dve", Vtok.v.re("p a b -> p (a b)"), pbt16[:, 0:512])
            ywritten = [False] * 4
            for d in dirs:
                Gi = G[d][:, 1:513].re("p (n t) -> p n t", t=128)
                Ge = G[d][:, 0:512].re("p (n t) -> p n t", t=128)
                Gbase = Ge[:, :, 0:1].bc([128, 4, 128])
                Gend = Gi[:, :, 127:128].bc([128, 4, 128])
                LA, LB, LC = f32t[0], f32t[1], f32t[2]
                v3 = lambda t_: t_.v.re("p (n t) -> p n t", t=128)
                Eo, Ex, Ee = f32t[3], f32t[0], f32t[1]
                if d == 0:
                    self.tt("dve", v3(LA), Gi, Gbase, ALU.subtract)
                    self.act(Ein[d].v, LA.v, AF.Exp)
                    self.act(Eo.v, LA.v, AF.Exp, scale=-1.0)
                    self.tt("dve", v3(LB), Ge, Gbase, ALU.subtract)
                    self.tt("dve", v3(LC), Gi, Gend, ALU.subtract)
                    sB, sC = 1.0, -1.0
                else:
                    self.tt("dve", v3(LA), Ge, Gend, ALU.subtract)
                    self.act(Ein[d].v, LA.v, AF.Exp, scale=-1.0)
                    self.act(Eo.v, LA.v, AF.Exp)
                    self.tt("dve", v3(LB), Gi, Gend, ALU.subtract)
                    self.tt("dve", v3(LC), Ge, Gbase, ALU.subtract)
                    sB, sC = -1.0, 1.0
                self.tt("dve", rW[d].v, r_sb.v, Ein[d].v, ALU.mult)
                self.tt("dve", kW[d].v, kd[d].v, Eo.v, ALU.mult)
                self.tt("dve", bW[d].v, bb[d].v, Eo.v, ALU.mult)
                self.act(Ex.v, LB.v, AF.Exp, scale=sB)
                self.tt("dve", kkW[d].v, kk.v, Ex.v, ALU.mult)
                self.act(Ee.v, LC.v, AF.Exp, scale=sC)
                self.tt("dve", fmt[0].v, kd[d].v, Ee.v, ALU.mult)
                self.stt(fmt[1].v, bb[d].v, -1.0, Ee.v, ALU.mult, ALU.mult)
                for (srcf, dstt) in ((fmt[0], kWct[d]), (fmt[1], nbWct[d])):
                    pbt = self.nb()
                    pbt16 = pbt.v.bit(BF16)
                    for n in range(4):
                        self.tr(pbt16[:, n * 128:(n + 1) * 128], srcf[:, n * 128:(n + 1) * 128], ident)
                    self.cp("act", dstt.v.re("p a b -> p (a b)"), pbt16[:, 0:512])
                if KCUT == 3:
                    continue
                if d == 0:
                    mS_N, mS_A, mI, mnI = self.mSU, self.mSL, self.mUI, self.mnUI
                else:
                    mS_N, mS_A, mI, mnI = self.mSL, self.mSU, self.mLI, self.mnLI

                def unit_mm(dst, lh, rh, mask, neg_ident=None):
                    for half in range(2):
                        pb_ = self.nb()
                        for u in range(4):
                            n, hh = u, half
                            ps_ = slice(hh * 64, (hh + 1) * 64)
                            self.mm(pb_[:, u * 128:(u + 1) * 128], lh[ps_, n * 128:(n + 1) * 128],
                                    rh[ps_, n * 128:(n + 1) * 128])
                        self.tt("dve", dst.h[half].v,
                                pb_.v.re("p (u t) -> p u t", t=128),
                                mask.re("p (o t) -> p o t", o=1).bc([128, 4, 128]), ALU.mult)

                unit_mm(tb["N0"], bW[d], kkW[d], mS_N)
                unit_mm(tb["A0"], kkW[d], bW[d], mS_A)
                unit_mm(AakT[d], kW[d], kkW[d], mS_N)
                unit_mm(ArkT[d], kW[d], rW[d], mI)
                unit_mm(nArbT[d], bW[d], rW[d], mnI)
                if KCUT == 4:
                    continue
                pA = self.patL if d == 0 else self.patU
                pN = self.patU if d == 0 else self.patL
                b4 = lambda m: m.re("p (o t) -> p o t", o=1).bc([128, 4, 128])

                def tbuild(hf, d=d, pA=pA, pN=pN):
                    ev = "act" if hf == 0 else "dve"
                    Aoffs = [tb["P0"].h[hf], tb["P1"].h[hf]]
                    Zb, nTt, TTd = tb["A1"].h[hf], tb["N1"].h[hf], TT[d].h[hf]
                    A0h, N0h = tb["A0"].h[hf], tb["N0"].h[hf]
                    self.tt("dve", Aoffs[0].v, N0h.v, b4(pN[0]), ALU.mult)
                    self.stt(TTd.v, Aoffs[0].v, -1.0, b4(ident), ALU.mult, ALU.add)
                    self.tt("dve", Zb.v, A0h.v, b4(pA[0]), ALU.mult)
                    self.stt(nTt.v, b4(ident), -1.0, Zb.v, ALU.mult, ALU.add)
                    self.tt("dve", Aoffs[1].v, A0h.v, b4(pA[1]), ALU.mult)
                    yield
                    for li in range(1, 7):
                        Aoff = Aoffs[li % 2]
                        pb_ = self.nb()
                        for u in range(4):
                            self.mm(pb_[:, u * 128:(u + 1) * 128], Aoff[:, u, :], TTd[:, u, :])
                        self.cp(ev, Zb.v, pb_.v.re("p (u t) -> p u t", t=128))
                        if li < 6:
                            self.tt("dve", Aoffs[(li + 1) % 2].v, A0h.v, b4(pA[li + 1]), ALU.mult)
                        yield
                        pb_ = self.nb()
                        for u in range(4):
                            self.mm(pb_[:, u * 128:(u + 1) * 128], ident, TTd[:, u, :], start=True, stop=False)
                            self.mm(pb_[:, u * 128:(u + 1) * 128], nTt[:, u, :], Zb[:, u, :], start=False, stop=True)
                        self.cp(ev, TTd.v, pb_.v.re("p (u t) -> p u t", t=128))
                        yield
                        if li < 6:
                            pbt = self.nb()
                            pbt16 = pbt.v.bit(BF16)
                            for u in range(4):
                                self.tr(pbt16[:, u * 128:(u + 1) * 128], TTd[:, u, :], ident)
                            if hf == 0:
                                self.act(nTt.v.re("p a b -> p (a b)"), pbt16[:, 0:512], AF.Copy, scale=-1.0)
                            else:
                                self.ts("dve", nTt.v.re("p a b -> p (a b)"), pbt16[:, 0:512], -1.0, ALU.mult)
                            yield

                gens = [tbuild(0), tbuild(1)]
                while gens:
                    for gobj in list(gens):
                        try:
                            next(gobj)
                        except StopIteration:
                            gens.remove(gobj)

                if KCUT == 5:
                    continue
                def chain(si, chunks, init, fin, d=d):
                    M = R["M"][si][d][cc]
                    mi = 0
                    if init == "zero":
                        self.memset("dve", M.v, 0.0)
                        self.memset("dve", Mb[si][0].v, 0.0)
                    else:
                        self.cp("dve", Mb[si][0].v, M.v)
                    yield
                    for n in chunks:
                        tok = slice(n * 128, (n + 1) * 128)
                        mb = Mb[si][mi]
                        pR = self.nb()
                        for hh in range(2):
                            ps_ = slice(hh * 64, (hh + 1) * 64)
                            self.mm(pR[:, hh * 64:(hh + 1) * 64], kkW[d][ps_, tok], mb[ps_, :], start=True, stop=False)
                            self.mm(pR[:, hh * 64:(hh + 1) * 64], AakT[d].u(hh * 4 + n),
                                    Vtok[:, n, hh * 64:(hh + 1) * 64], start=False, stop=True)
                        self.cp("act", Rb[si].v, pR[:, 0:128])
                        yield
                        pU = self.nb()
                        for hh in range(2):
                            self.mm(pU[:, hh * 64:(hh + 1) * 64], TT[d].u(hh * 4 + n), Rb[si][:, hh * 64:(hh + 1) * 64])
                        self.cp("dve", Ub[si].v, pU[:, 0:128])
                        yield
                        pM = self.nb()
                        for hh in range(2):
                            ps_ = slice(hh * 64, (hh + 1) * 64)
                            self.mm(pM[ps_, 0:64], kWct[d][:, n, hh * 64:(hh + 1) * 64],
                                    Vtok[:, n, hh * 64:(hh + 1) * 64], start=True, stop=False)
                            self.mm(pM[ps_, 0:64], nbWct[d][:, n, hh * 64:(hh + 1) * 64],
                                    Ub[si][:, hh * 64:(hh + 1) * 64], start=False, stop=True)
                        pY = self.nb()
                        for hh in range(2):
                            ps_ = slice(hh * 64, (hh + 1) * 64)
                            self.mm(pY[ps_, 0:128], mb[ps_, :], rW[d][ps_, tok], start=True, stop=False)
                            self.mm(pY[ps_, 0:128], Vtok[:, n, hh * 64:(hh + 1) * 64], ArkT[d].u(hh * 4 + n),
                                    start=False, stop=False)
                            self.mm(pY[ps_, 0:128], Ub[si][:, hh * 64:(hh + 1) * 64], nArbT[d].u(hh * 4 + n),
                                    start=False, stop=True)
                        dcol = n * 128 + (127 if d == 0 else 0)
                        self.stt(M.v, M.v, Ein[d][:, dcol:dcol + 1], pM[:, 0:64], ALU.mult, ALU.add)
                        mi ^= 1
                        self.cp("dve", Mb[si][mi].v, M.v)
                        if pa["store"] is not None:
                            self.cp("act", R["yfs"][pa["store"]][cc][:, tok], pY[:, 0:128])
                        elif pa["load"] is not None:
                            self.tt("dve", yacc[:, tok], pY[:, 0:128], R["yfs"][pa["load"]][cc][:, tok], ALU.add)
                        elif not ywritten[n]:
                            self.cp("act", yacc[:, tok], pY[:, 0:128])
                            ywritten[n] = True
                        else:
                            self.tt("dve", yacc[:, tok], pY[:, 0:128], yacc[:, tok], ALU.add)
                        yield
                    if fin is not None:
                        self.dma_out("sp", self.d_ns[d, fin, cc], M.v, lane=self.lane_ns[d])
                        yield

                gens = [chain(si, *cfg) for si, cfg in enumerate(pa["chains"][d])]
                if d == dirs[-1] and cc + 1 < 8 and KCUT > 5:
                    gens.append(proj_gen(cc + 1))
                while gens:
                    for gobj in list(gens):
                        try:
                            next(gobj)
                        except StopIteration:
                            gens.remove(gobj)

            if final and KCUT > 6:
                pm = self.nb()
                self.mm(pm.v, self.bonesf, yacc.v)
                yc = f32t[0]
                self.stt(yc.v, pm.v, -1.0 / 64, yacc.v, ALU.mult, ALU.add)
                sq2 = f32t[1]
                self.act(sq2.v, yc.v, AF.Square)
                pv2 = self.nb()
                self.mm(pv2.v, self.bonesf, sq2.v)
                rs = f32t[2]
                self.rsqrt(rs.v, pv2.v, self.vcol("eps", 1), scale=1.0 / 64)
                self.tt("dve", yc.v, yc.v, rs.v, ALU.mult)
                zz = f32t[3]
                self.act(zz.v, yc.v, AF.Identity, bias=self.vcol("gn_b", cc), scale=self.vcol("gn_w", cc))
                pbn = self.nb()
                for d2 in range(2):
                    self.stt(fmt[d2].v, r_sb.v, self.vcol("r_k", cc), kd[d2].v, ALU.mult, ALU.mult)
                    self.mm(pbn.v, self.bones, fmt[d2].v, start=(d2 == 0), stop=(d2 == 1))
                bon = f32t[1]
                self.tt("dve", bon.v, pbn.v, v_sb.v, ALU.mult)
                self.tt("dve", zz.v, zz.v, bon.v, ALU.add)
                pg = self.nb()
                self.mm(pg.v, R["g2"][:, cc * 128:(cc + 1) * 128], sg.v)
                self.tt("dve", z[cc].v, zz.v, pg.v, ALU.mult)

        if final and KCUT > 7:
            wod = self.d_wo.rearrange("(kc p) n -> p kc n", p=128)
            wts = [wraw[0][0], wraw[0][1], wraw[0][2], wraw[1][0], wraw[1][1], wraw[1][2]]
            for dc in range(KC):
                wt = wts[dc % 6]
                self.dma_in("pool", wt.v, wod[:, :, dc * 128:(dc + 1) * 128])
                pb = self.nb()
                for kc in range(KC):
                    self.mm(pb.v, wt[:, kc, :], z[kc].v, start=(kc == 0), stop=(kc == KC - 1))
                self.cp("dve", osb[dc], pb.v)
            sqs = [z[c].v for c in range(KC)]
            self.rms_rstd(osb, sqs, self.nb().v, rstd.v)
            for c in range(KC):
                tm = f32t[c % 2]
                self.tt("dve", tm.v, osb[c], rstd.v, ALU.mult)
                self.stt(self.XT[c][tt].v, tm.v, self.gp(l, s, c, q), self.XT[c][tt].v, ALU.mult, ALU.add)


def _vec8(v):
    v = np.asarray(v, np.float32)
    lead = int(np.prod(v.shape[:-1])) if v.ndim > 1 else 1
    return np.ascontiguousarray(v.reshape(lead, 8, 128).transpose(2, 0, 1).reshape(128, lead * 8))


def _consts():
    cst = np.zeros((128, CST_N), np.float32)
    i = np.arange(128)
    cst[:, 0:128] = np.eye(128)
    cst[:, 128:256] = 1.0
    cst[:, 256:384] = (i[:, None] // 64 == i[None, :] // 64)
    cst[:, 384:512] = (i[:, None] < i[None, :])
    cst[:, 512:640] = (i[:, None] > i[None, :])
    cst[:, 640:768] = (i[:, None] <= i[None, :])
    cst[:, 768:896] = (i[:, None] >= i[None, :])
    for bi_, B in enumerate((1, 2, 4, 8, 16, 32, 64)):
        t_, s_ = i[:, None], i[None, :]
        pb_ = ((t_ // (2 * B)) == (s_ // (2 * B))) & ((t_ % (2 * B)) >= B) & ((s_ % (2 * B)) < B)
        cst[:, (7 + bi_) * 128:(8 + bi_) * 128] = pb_
        cst[:, (14 + bi_) * 128:(15 + bi_) * 128] = pb_.T
    pc = np.zeros((4, 256 + 32 + 64), np.float32)
    for wi, w in enumerate(POOL_WINDOWS):
        for off, L in ((0, 256), (256, 32), (288, 64)):
            t = np.arange(L)
            lo = np.clip(t - w // 2, 0, L)
            hi = np.clip(t + w - w // 2, 0, L)
            pc[wi, off:off + L] = 1.0 / (hi - lo)
    pcst = np.ascontiguousarray(np.broadcast_to(pc.reshape(1, -1), (128, pc.size))).astype(np.float32)
    return cst, pcst


_STOP = None
KCUT = int(os.environ.get('KCUT', '99'))
KPASS = int(os.environ.get('KPASS', '99'))
K1CORE = int(os.environ.get('K1CORE', '0'))
_NC_CACHE = {}


def kernel(x_prompt, x_sample, c, state_ctx_fwd, state_ctx_bwd, c_ctx, w_mod, b_mod, norm_pre, norm_post,
           ffn_w1, ffn_w3, ffn_w2, rwkv_mu, rwkv_w_rkv, rwkv_w0, rwkv_w1, rwkv_w2, rwkv_a0, rwkv_a1,
           rwkv_a2, rwkv_g1, rwkv_g2, rwkv_k_k, rwkv_k_a, rwkv_r_k, rwkv_gn_w, rwkv_gn_b, rwkv_w_o,
           pool_w, pool_scale):
    f = lambda a: np.ascontiguousarray(np.asarray(a, np.float32))
    ncores = 8
    key = _STOP
    if key not in _NC_CACHE:
        kb = KB(stop=_STOP)
        _NC_CACHE[key] = kb.build()
        _NC_CACHE["kb"] = kb
    nc = _NC_CACHE[key]
    cst, pcst = _consts()
    shared = {
        "cst": cst, "pcst": pcst,
        "w_mod": f(w_mod), "ffn_w1": f(ffn_w1), "ffn_w3": f(ffn_w3), "ffn_w2": f(ffn_w2),
        "rwkv_w_rkv": f(rwkv_w_rkv[0]), "rwkv_w1": f(rwkv_w1[0]), "rwkv_w2": f(rwkv_w2[0]),
        "rwkv_a1": f(rwkv_a1[0]), "rwkv_a2": f(rwkv_a2[0]), "rwkv_g1": f(rwkv_g1[0]),
        "rwkv_g2": f(rwkv_g2[0]), "rwkv_w_o": f(rwkv_w_o[0]), "pool_w": f(pool_w[0]),
    }
    vec_common = {
        "npre": _vec8(norm_pre), "npost": _vec8(norm_post),
        "bmod": np.ascontiguousarray(np.asarray(b_mod, np.float32).reshape(2, 72, 128).transpose(2, 0, 1).reshape(128, 144)),
        "mu": _vec8(rwkv_mu[0]), "w0": _vec8(rwkv_w0[0]), "a0": _vec8(rwkv_a0[0]),
        "k_k": _vec8(rwkv_k_k[0]), "k_a": _vec8(rwkv_k_a[0]), "r_k": _vec8(np.asarray(rwkv_r_k[0]).reshape(-1)),
        "gn_w": _vec8(rwkv_gn_w[0]), "gn_b": _vec8(rwkv_gn_b[0]), "pscale": _vec8(pool_scale[0]),
    }
    eps = np.zeros((128, 8), np.float32)
    eps[:, 0] = RMS_EPS
    eps[:, 1] = GN_EPS
    eps[:, 2] = 1e-24
    in_maps = []
    x_sample = np.asarray(x_sample, np.float32)
    x_prompt = np.asarray(x_prompt, np.float32)
    for b in range(ncores):
        xT = np.empty((D, TTOT), np.float32)
        xT[:, :TS] = x_sample[b].T
        xT[:, TS:] = x_prompt[4 * b:4 * b + 4].reshape(TP, D).T
        vecs = np.zeros((128, NV), np.float32)
        for n, arr in vec_common.items():
            vecs[:, VEC_OFF[n]:VEC_OFF[n] + arr.shape[1]] = arr
        cond = np.stack([np.asarray(c[b], np.float32), np.asarray(c_ctx, np.float32)], axis=0)
        vecs[:, VEC_OFF["cond"]:VEC_OFF["cond"] + 16] = cond.reshape(2, 8, 128).transpose(2, 1, 0).reshape(128, 16)
        vecs[:, VEC_OFF["eps"]:VEC_OFF["eps"] + 8] = eps
        s0 = np.stack([np.asarray(state_ctx_fwd[b, 0], np.float32), np.asarray(state_ctx_bwd[b, 0], np.float32)])
        s0 = s0.transpose(0, 1, 3, 2).reshape(2, 8, 128, 64)
        m = dict(shared)
        m["xT"] = xT
        m["vecs"] = vecs
        m["s0"] = np.ascontiguousarray(s0)
        in_maps.append(m)
    if K1CORE:
        res1 = run_bass_kernel_spmd(nc, in_maps[:1], core_ids=[0])
        class _R:
            pass
        res = _R()
        res.results = [res1.results[0]] * ncores
    else:
        res = run_bass_kernel_spmd(nc, in_maps, core_ids=list(range(ncores)))
    y_sample = np.empty((8, TS, D), np.float32)
    y_prompt = np.empty((32, 256, D), np.float32)
    nsf = np.empty((32, 1, 16, 64, 64), np.float32)
    nsb = np.empty((32, 1, 16, 64, 64), np.float32)
    for b in range(ncores):
        r = res.results[b]
        yT = r["yT"]
        y_sample[b] = yT[:, :TS].T
        y_prompt[4 * b:4 * b + 4] = yT[:, TS:].T.reshape(4, 256, D)
        ns = r["ns"].reshape(2, 4, 16, 64, 64).transpose(0, 1, 2, 4, 3)
        nsf[4 * b:4 * b + 4, 0] = ns[0]
        nsb[4 * b:4 * b + 4, 0] = ns[1]
    return (y_prompt, y_sample, nsf, nsb)
```
